# Optimizing a Trainium2 kernel written in Bass

```python
import jax, jax.numpy as jnp
from jax import lax
import numpy as np

D_MODEL = 1024
BATCH = 2
SEQ = 8192
DEPTH = 4

HEAD_DIM = 64
ROPE_THETA = 10000.0
RMS_EPS = 1e-6
QBLK = 128
NEG_INF = -1e30

GRID_W = 64
NA_HEADS = 8
NA_ROWS = 8
NA_COLS = 16
NA_WIDTH = NA_HEADS * HEAD_DIM

MLA_HEADS = 8
MLA_Q_RANK = 256
MLA_KV_RANK = 128
MLA_NOPE = 64
MLA_ROPE = 32
MLA_V = 64

DIL_PAIRS = ((128, 1), (512, 4), (2048, 16))
DIL_GROUPS = len(DIL_PAIRS)
DIL_HEADS = D_MODEL // HEAD_DIM

D_FF = 4 * D_MODEL
N_EVEN = (DEPTH + 1) // 2
N_ODD = DEPTH // 2

EVEN_IN = 3 * NA_WIDTH + MLA_Q_RANK + MLA_KV_RANK + MLA_ROPE
EVEN_MIX = NA_WIDTH + MLA_HEADS * MLA_V
ODD_IN = DIL_GROUPS * 3 * DIL_HEADS * HEAD_DIM
ODD_MIX = DIL_HEADS * HEAD_DIM

kernel_name = "hybrid_na_mla_dilated_encoder"


def rms_norm(x, g):
    xf = x.astype(jnp.float32)
    y = xf * lax.rsqrt(jnp.mean(xf * xf, axis=-1, keepdims=True) + RMS_EPS)
    return (y * g.astype(jnp.float32)).astype(x.dtype)


def rope_tables(seq_len, dim):
    inv = 1.0 / (ROPE_THETA ** (jnp.arange(0, dim, 2, dtype=jnp.float32) / dim))
    ang = jnp.arange(seq_len, dtype=jnp.float32)[:, None] * inv[None, :]
    return jnp.cos(ang), jnp.sin(ang)


def apply_rope(x, cos, sin):
    half = x.shape[-1] // 2
    shp = (1, cos.shape[0]) + (1,) * (x.ndim - 3) + (cos.shape[1],)
    c = cos.reshape(shp).astype(x.dtype)
    s = sin.reshape(shp).astype(x.dtype)
    x1, x2 = x[..., :half], x[..., half:]
    return jnp.concatenate([x1 * c - x2 * s, x2 * c + x1 * s], axis=-1)


def neighbourhood_attention(q, k, v, rpb):
    b, s, h, dh = q.shape
    rows = s // GRID_W
    wr = min(NA_ROWS, rows)
    wc = NA_COLS
    qg = q.reshape(b, rows, GRID_W, h, dh)
    kg = k.reshape(b, rows, GRID_W, h, dh)
    vg = v.reshape(b, rows, GRID_W, h, dh)
    cols = jnp.arange(GRID_W)
    col_start = jnp.clip(cols - wc // 2, 0, GRID_W - wc)
    col_idx = col_start[:, None] + jnp.arange(wc)[None, :]
    col_rel = col_idx - cols[:, None] + (NA_COLS - 1)
    scale = dh ** -0.5

    def row_fn(r):
        r_start = jnp.clip(r - wr // 2, 0, rows - wr)
        row_rel = r_start + jnp.arange(wr) - r + (NA_ROWS - 1)
        q_row = lax.dynamic_index_in_dim(qg, r, axis=1, keepdims=False)
        k_rows = lax.dynamic_slice_in_dim(kg, r_start, wr, axis=1)
        v_rows = lax.dynamic_slice_in_dim(vg, r_start, wr, axis=1)
        k_win = k_rows[:, :, col_idx]
        v_win = v_rows[:, :, col_idx]
        sc = jnp.einsum('bchd,bicjhd->bhcij', q_row, k_win,
                        preferred_element_type=jnp.float32) * scale
        bias = rpb[:, row_rel[:, None, None], col_rel[None, :, :]]
        sc = sc + bias.transpose(0, 2, 1, 3).astype(jnp.float32)[None]
        p = jax.nn.softmax(sc.reshape(b, h, GRID_W, wr * wc), axis=-1).reshape(sc.shape)
        return jnp.einsum('bhcij,bicjhd->bchd', p.astype(v.dtype), v_win)

    out = lax.map(row_fn, jnp.arange(rows))
    return out.transpose(1, 0, 2, 3, 4).reshape(b, s, h * dh)


def dense_attention(q, k, v, scale):
    b, s, h, dq = q.shape
    nblk = s // QBLK
    qb = q.reshape(b, nblk, QBLK, h, dq).transpose(1, 0, 2, 3, 4)

    def blk(qi):
        sc = jnp.einsum('bqhd,bkhd->bhqk', qi, k, preferred_element_type=jnp.float32) * scale
        p = jax.nn.softmax(sc, axis=-1)
        return jnp.einsum('bhqk,bkhd->bqhd', p.astype(v.dtype), v)

    out = lax.map(blk, qb)
    return out.transpose(1, 0, 2, 3, 4).reshape(b, s, h * v.shape[-1])


def dilated_attention(q, k, v):
    b, s, g, h, dh = q.shape
    nblk = s // QBLK
    scale = dh ** -0.5
    offsets = [jnp.arange(-(w // 2), w // 2 + 1, d) for (w, d) in DIL_PAIRS]
    ks = [k[:, :, gi] for gi in range(g)]
    vs = [v[:, :, gi] for gi in range(g)]

    def blk(bi):
        start = bi * QBLK
        qpos = start + jnp.arange(QBLK)
        q_blk = lax.dynamic_slice_in_dim(q, start, QBLK, axis=1)
        outs, lses = [], []
        for gi in range(g):
            kpos = qpos[:, None] + offsets[gi][None, :]
            valid = (kpos >= 0) & (kpos < s)
            kidx = jnp.clip(kpos, 0, s - 1)
            k_sel = ks[gi][:, kidx]
            v_sel = vs[gi][:, kidx]
            sc = jnp.einsum('bqhd,bqjhd->bhqj', q_blk[:, :, gi], k_sel,
                            preferred_element_type=jnp.float32) * scale
            sc = jnp.where(valid[None, None], sc, NEG_INF)
            m = jnp.max(sc, axis=-1, keepdims=True)
            e = jnp.exp(sc - m)
            z = jnp.sum(e, axis=-1, keepdims=True)
            o = jnp.einsum('bhqj,bqjhd->bqhd', (e / z).astype(v.dtype), v_sel)
            outs.append(o.astype(jnp.float32))
            lses.append((m + jnp.log(z))[..., 0])
        lse = jnp.stack(lses, axis=0)
        wgt = jax.nn.softmax(lse, axis=0).transpose(0, 1, 3, 2)[..., None]
        out = jnp.sum(wgt * jnp.stack(outs, axis=0), axis=0)
        return out.astype(v.dtype)

    out = lax.map(blk, jnp.arange(nblk))
    return out.transpose(1, 0, 2, 3, 4).reshape(b, s, h * dh)


def even_mixer(xn, w_in, rpb, q_norm, w_uq, kv_norm, w_ukv, w_o, cos_r, sin_r):
    b, s, _ = xn.shape
    hcat = xn @ w_in
    o0 = 3 * NA_WIDTH
    o1 = o0 + MLA_Q_RANK
    o2 = o1 + MLA_KV_RANK
    a_qkv = hcat[..., :o0].reshape(b, s, 3, NA_HEADS, HEAD_DIM)
    out_a = neighbourhood_attention(a_qkv[:, :, 0], a_qkv[:, :, 1], a_qkv[:, :, 2], rpb)
    c_q = rms_norm(hcat[..., o0:o1], q_norm)
    c_kv = rms_norm(hcat[..., o1:o2], kv_norm)
    k_pe = apply_rope(hcat[..., o2:][:, :, None, :], cos_r, sin_r)
    q = (c_q @ w_uq).reshape(b, s, MLA_HEADS, MLA_NOPE + MLA_ROPE)
    q = jnp.concatenate([q[..., :MLA_NOPE], apply_rope(q[..., MLA_NOPE:], cos_r, sin_r)], axis=-1)
    kv = (c_kv @ w_ukv).reshape(b, s, MLA_HEADS, MLA_NOPE + MLA_V)
    k = jnp.concatenate([kv[..., :MLA_NOPE],
                         jnp.broadcast_to(k_pe, (b, s, MLA_HEADS, MLA_ROPE))], axis=-1)
    v = kv[..., MLA_NOPE:]
    out_b = dense_attention(q, k, v, (MLA_NOPE + MLA_ROPE) ** -0.5)
    return jnp.concatenate([out_a, out_b], axis=-1) @ w_o


def odd_mixer(xn, w_in, w_o, cos_f, sin_f):
    b, s, _ = xn.shape
    hcat = (xn @ w_in).reshape(b, s, DIL_GROUPS, 3, DIL_HEADS, HEAD_DIM)
    q = apply_rope(hcat[:, :, :, 0], cos_f, sin_f)
    k = apply_rope(hcat[:, :, :, 1], cos_f, sin_f)
    v = hcat[:, :, :, 2]
    return dilated_attention(q, k, v) @ w_o


def sq_relu_mlp(xn, w1, w2):
    return jnp.square(jax.nn.relu(xn @ w1)) @ w2


def setup_inputs(seed: int = 0) -> dict:
    key = jax.random.key(seed)
    ks = jax.random.split(key, 16)
    f32 = jnp.float32

    def nrm(k, shape, scale):
        return jax.random.normal(k, shape, f32) * scale

    return {
        "x": nrm(ks[0], (BATCH, SEQ, D_MODEL), 1.0),
        "norm_mix": 1.0 + nrm(ks[1], (DEPTH, D_MODEL), 0.1),
        "norm_mlp": 1.0 + nrm(ks[2], (DEPTH, D_MODEL), 0.1),
        "norm_final": 1.0 + nrm(ks[3], (D_MODEL,), 0.1),
        "ev_w_in": nrm(ks[4], (N_EVEN, D_MODEL, EVEN_IN), D_MODEL ** -0.5),
        "ev_rpb": nrm(ks[5], (N_EVEN, NA_HEADS, 2 * NA_ROWS - 1, 2 * NA_COLS - 1), 0.5),
        "ev_q_norm": 1.0 + nrm(ks[6], (N_EVEN, MLA_Q_RANK), 0.1),
        "ev_w_uq": nrm(ks[7], (N_EVEN, MLA_Q_RANK, MLA_HEADS * (MLA_NOPE + MLA_ROPE)), MLA_Q_RANK ** -0.5),
        "ev_kv_norm": 1.0 + nrm(ks[8], (N_EVEN, MLA_KV_RANK), 0.1),
        "ev_w_ukv": nrm(ks[9], (N_EVEN, MLA_KV_RANK, MLA_HEADS * (MLA_NOPE + MLA_V)), MLA_KV_RANK ** -0.5),
        "ev_w_o": nrm(ks[10], (N_EVEN, EVEN_MIX, D_MODEL), EVEN_MIX ** -0.5),
        "od_w_in": nrm(ks[11], (N_ODD, D_MODEL, ODD_IN), D_MODEL ** -0.5),
        "od_w_o": nrm(ks[12], (N_ODD, ODD_MIX, D_MODEL), ODD_MIX ** -0.5),
        "mlp_w1": nrm(ks[13], (DEPTH, D_MODEL, D_FF), D_MODEL ** -0.5),
        "mlp_w2": nrm(ks[14], (DEPTH, D_FF, D_MODEL), D_FF ** -0.5),
    }


def reference(x, norm_mix, norm_mlp, norm_final, ev_w_in, ev_rpb, ev_q_norm, ev_w_uq,
              ev_kv_norm, ev_w_ukv, ev_w_o, od_w_in, od_w_o, mlp_w1, mlp_w2):
    s = x.shape[1]
    cos_r, sin_r = rope_tables(s, MLA_ROPE)
    cos_f, sin_f = rope_tables(s, HEAD_DIM)
    for layer in range(DEPTH):
        xn = rms_norm(x, norm_mix[layer])
        if layer % 2 == 0:
            e = layer // 2
            x = x + even_mixer(xn, ev_w_in[e], ev_rpb[e], ev_q_norm[e], ev_w_uq[e],
                               ev_kv_norm[e], ev_w_ukv[e], ev_w_o[e], cos_r, sin_r)
        else:
            o = layer // 2
            x = x + odd_mixer(xn, od_w_in[o], od_w_o[o], cos_f, sin_f)
        x = x + sq_relu_mlp(rms_norm(x, norm_mlp[layer]), mlp_w1[layer], mlp_w2[layer])
    return rms_norm(x, norm_final)
```

```python
import math
from contextlib import ExitStack

import numpy as np
import ml_dtypes
import concourse.bass as bass
import concourse.mybir as mybir
from concourse.bass_utils import run_bass_kernel_spmd

F32 = mybir.dt.float32
BF16 = mybir.dt.bfloat16
AF = mybir.ActivationFunctionType
ALU = mybir.AluOpType

ENGS = ("pe", "act", "dve", "pool", "sp")
NDMA_SEMS = 12

D = 1024
T = 2048
S = 8192
NCORES = 8
HALO = 1024
EXT = T + 2 * HALO
EPS = 1e-6
NEG = -30000.0
DIL = ((128, 1), (512, 4), (2048, 16))
ARENA_ELEMS = 103 * 1024


class Op:
    __slots__ = ("eng", "fn", "reads", "writes", "dma", "deps", "sem", "semval",
                 "has_dep", "prewait", "barrier")

    def __init__(self, eng, fn, reads, writes, dma):
        self.eng = eng
        self.fn = fn
        self.reads = reads
        self.writes = writes
        self.dma = dma
        self.deps = set()
        self.has_dep = False
        self.sem = None
        self.semval = 0
        self.prewait = None
        self.barrier = False


class Prog:
    def __init__(self, nc):
        self.nc = nc
        self.ops = []
        self.last_w = {}
        self.readers = {}
        self.last_op = {}
        self.dmas_since_barrier = []

    def op(self, eng, fn, reads=(), writes=(), dma=False):
        o = Op(eng, fn, tuple(reads), tuple(writes), dma)
        idx = len(self.ops)
        deps = set()
        for r in o.reads:
            w = self.last_w.get(r)
            if w is not None:
                deps.add(w)
        for wkey in o.writes:
            w = self.last_w.get(wkey)
            if w is not None:
                deps.add(w)
            for rd in self.readers.get(wkey, ()):
                deps.add(rd)
        deps.discard(idx)
        if eng == "pe" and not dma:
            deps = {d for d in deps if not (self.ops[d].eng == "pe" and not self.ops[d].dma)}
        o.deps = deps
        for d in deps:
            self.ops[d].has_dep = True
        for r in o.reads:
            self.readers.setdefault(r, []).append(idx)
        for wkey in o.writes:
            self.last_w[wkey] = idx
            self.readers[wkey] = []
        self.ops.append(o)
        if dma:
            self.dmas_since_barrier.append(idx)
        else:
            self.last_op[eng] = idx
        return idx

    def pe(self, fn, reads=(), writes=()):
        return self.op("pe", fn, reads, writes)

    def act(self, fn, reads=(), writes=()):
        return self.op("act", fn, reads, writes)

    def dve(self, fn, reads=(), writes=()):
        return self.op("dve", fn, reads, writes)

    def pool(self, fn, reads=(), writes=()):
        return self.op("pool", fn, reads, writes)

    def dma(self, eng, fn, reads=(), writes=()):
        return self.op(eng, fn, reads, writes, dma=True)

    def barrier(self):
        deps = set(self.last_op.values()) | set(self.dmas_since_barrier)
        for d in deps:
            self.ops[d].has_dep = True
        for e in ENGS:
            o = Op(e, None, (), (), False)
            o.barrier = True
            o.deps = set(deps)
            self.ops.append(o)
        self.last_w = {}
        self.readers = {}
        self.dmas_since_barrier = []

    def emit(self, final_wait_ops=()):
        nc = self.nc
        with ExitStack() as es:
            esem = {e: es.enter_context(nc.semaphore("s_" + e)) for e in ENGS}
            dsem = {e: [es.enter_context(nc.semaphore("d_%s%d" % (e, i))) for i in range(NDMA_SEMS)]
                    for e in ("sp", "pool", "act")}
            ecount = {e: 0 for e in ENGS}
            dcount = {e: 0 for e in dsem}
            dval = {e: [0] * NDMA_SEMS for e in dsem}
            for o in self.ops:
                if o.barrier:
                    continue
                if o.dma:
                    k = dcount[o.eng]
                    slot = k % NDMA_SEMS
                    dcount[o.eng] += 1
                    o.sem = dsem[o.eng][slot]
                    if dval[o.eng][slot] > 0:
                        o.prewait = (o.sem, dval[o.eng][slot])
                    dval[o.eng][slot] += 16
                    o.semval = dval[o.eng][slot]
                elif o.has_dep:
                    ecount[o.eng] += 1
                    o.sem = esem[o.eng]
                    o.semval = ecount[o.eng]
            streams = {e: [] for e in ENGS}
            for i, o in enumerate(self.ops):
                streams[o.eng].append(i)
            final_waits = [(self.ops[i].sem, self.ops[i].semval) for i in final_wait_ops]
            ops = self.ops
            self.stats = {e: len(streams[e]) for e in ENGS}
            self.stats["sem_max"] = dict(ecount)

            def run_stream(e, eng):
                waited = {}
                for i in streams[e]:
                    o = ops[i]
                    need = {}
                    if o.prewait is not None:
                        need[id(o.prewait[0])] = o.prewait
                    for d in o.deps:
                        do = ops[d]
                        key = id(do.sem)
                        cur = need.get(key)
                        if cur is None or cur[1] < do.semval:
                            need[key] = (do.sem, do.semval)
                    for key, (s, v) in need.items():
                        if waited.get(key, 0) >= v:
                            continue
                        eng.wait_ge(s, v)
                        waited[key] = v
                    if o.barrier:
                        continue
                    ins = o.fn(eng)
                    if o.sem is not None:
                        ins.then_inc(o.sem, 16 if o.dma else 1)
                if e == "sp":
                    for s, v in final_waits:
                        eng.wait_ge(s, v)

            with nc.Block() as block:
                @block.sync
                def _(eng):
                    run_stream("sp", eng)

                @block.tensor
                def _(eng):
                    run_stream("pe", eng)

                @block.scalar
                def _(eng):
                    run_stream("act", eng)

                @block.vector
                def _(eng):
                    run_stream("dve", eng)

                @block.gpsimd
                def _(eng):
                    run_stream("pool", eng)


class Arena:
    def __init__(self, tensor, nelem):
        self.t = tensor
        self.n = nelem
        self.top = 0

    def alloc(self, shape, dt):
        free = 1
        for s in shape[1:]:
            free *= s
        n16 = free * (2 if dt == F32 else 1)
        n16 = (n16 + 15) // 16 * 16
        off = self.top
        self.top += n16
        assert self.top <= self.n, ("SBUF arena overflow", self.top, self.n)
        v = self.t[:, off:off + free * (2 if dt == F32 else 1)]
        if dt == F32:
            v = v.bitcast(F32)
        if len(shape) == 3:
            v = v.rearrange("p (a b) -> p a b", a=shape[1])
        elif len(shape) == 4:
            v = v.rearrange("p (a b c) -> p a b c", a=shape[1], b=shape[2])
        return v

    def mark(self):
        return self.top

    def release(self, m):
        self.top = m


class Builder:
    def __init__(self, kind):
        self.kind = kind
        self.nc = bass.Bass("TRN2", target_bir_lowering=False)
        self.P = Prog(self.nc)
        self.dram = {}
        self.uid = 0
        self.rr = 0
        self.stop = None

    def din(self, name, shape, dt=F32):
        t = self.nc.dram_tensor(name, list(shape), dt, kind="ExternalInput").ap()
        self.dram[name] = t
        return t

    def dout(self, name, shape, dt=F32):
        t = self.nc.dram_tensor(name, list(shape), dt, kind="ExternalOutput").ap()
        self.dram[name] = t
        return t

    def dint(self, name, shape, dt=F32):
        t = self.nc.dram_tensor(name, list(shape), dt, kind="Internal").ap()
        self.dram[name] = t
        return t

    def key(self, base):
        self.uid += 1
        return "%s#%d" % (base, self.uid)

    def load_w(self, dst, src_ap, key):
        self.P.dma("pool", lambda e, d=dst, s=src_ap: e.dma_start(out=d, in_=s), writes=[key])

    def load(self, dst, src_ap, key, q="sp"):
        self.P.dma(q, lambda e, d=dst, s=src_ap: e.dma_start(out=d, in_=s), writes=[key])

    def evac(self, dst, src, reads, writes, which=None):
        if which is None:
            self.rr ^= 1
            which = "act" if self.rr else "dve"
        if which == "act":
            self.P.act(lambda e, d=dst, s=src: e.activation(out=d, in_=s, func=AF.Copy), reads, writes)
        else:
            self.P.dve(lambda e, d=dst, s=src: e.tensor_copy(out=d, in_=s), reads, writes)

    def rmsnorm_fm(self, A, xsrc, xkeys, gain, nfeat_chunks, ntok, dst, dkeys, ps_bank, ones,
                   dst32=None, blk=512, nfeat=None):
        P = self.P
        C = nfeat_chunks
        nfeat = nfeat or C * 128
        m = A.mark()
        sq = [A.alloc([128, C, blk], BF16) for _ in range(2)]
        sr = [A.alloc([128, blk], F32) for _ in range(2)]
        rr = [A.alloc([128, blk], F32) for _ in range(2)]
        ksq = [self.key("sq"), self.key("sq")]
        ksr = [self.key("sr"), self.key("sr")]
        krr = [self.key("rr"), self.key("rr")]
        pskey = "ps%d" % ps_bank
        for b in range(ntok // blk):
            i = b % 2
            sl = slice(b * blk, (b + 1) * blk)
            P.act(lambda e, o=sq[i], x=xsrc[:, :, sl]: e.activation(out=o, in_=x, func=AF.Square),
                  [xkeys[b]], [ksq[i]])
            for c in range(C):
                P.pe(lambda e, c=c, i=i: e.matmul(self.ps[:, ps_bank, 0:blk], ones[:, :], sq[i][:, c, :],
                                                 start=(c == 0), stop=(c == C - 1)),
                     [ksq[i], "ones"], [pskey])
            P.act(lambda e, o=sr[i]: e.activation(out=o, in_=self.ps[:, ps_bank, 0:blk], func=AF.Sqrt,
                                                 bias=self.eps_t[:, 0:1], scale=1.0 / nfeat),
                  [pskey], [ksr[i]])
            P.dve(lambda e, o=rr[i], s=sr[i]: e.reciprocal(out=o, in_=s), [ksr[i]], [krr[i]])
            for c in range(C):
                P.dve(lambda e, c=c, i=i, sl=sl: e.scalar_tensor_tensor(
                    out=dst[:, c, sl], in0=xsrc[:, c, sl], scalar=gain[:, c:c + 1], in1=rr[i],
                    op0=ALU.mult, op1=ALU.mult), [xkeys[b], krr[i], "gains"], [dkeys[b]])
                if dst32 is not None:
                    P.pool(lambda e, c=c, i=i, sl=sl: e.tensor_tensor(
                        out=dst32[:, c, sl], in0=xsrc[:, c, sl], in1=rr[i], op=ALU.mult),
                        [xkeys[b], krr[i]], [dkeys[b] + "f"])

    def build(self):
        nc = self.nc
        P = self.P
        kind = self.kind
        with ExitStack() as es:
            arena_t = es.enter_context(nc.sbuf_tensor("arena", [128, ARENA_ELEMS], BF16))
            self.ps = es.enter_context(nc.psum_tensor("ps", [128, 8, 512], F32))
            A = Arena(arena_t, ARENA_ELEMS)
            self.A = A
            self.eps_t = A.alloc([128, 1], F32)
            P.dve(lambda e: e.memset(self.eps_t, EPS), [], ["eps"])
            ones = A.alloc([128, 128], BF16)
            self.ones = ones
            P.dve(lambda e: e.memset(ones, 1.0), [], ["ones"])
            if kind == "norm0":
                outs = self.build_norm0(A)
            else:
                outs = self.build_layer(A)
            P.barrier()
            P.emit(final_wait_ops=outs)
        return nc

    def build_norm0(self, A):
        P = self.P
        xT = self.din("xT", [D, T])
        g = self.din("gnext", [128, 8])
        xn_o = self.dout("xn_bf", [D, T], BF16)
        gains = A.alloc([128, 8], F32)
        self.load(gains, g, "gains")
        xres = A.alloc([128, 8, T], F32)
        xkeys = ["xres%d" % b for b in range(4)]
        for b in range(4):
            self.load(xres[:, :, b * 512:(b + 1) * 512],
                      xT.rearrange("(c p) t -> p c t", p=128)[:, :, b * 512:(b + 1) * 512], xkeys[b])
        xn = A.alloc([128, 8, T], BF16)
        dkeys = ["xn%d" % b for b in range(4)]
        self.rmsnorm_fm(A, xres, xkeys, gains, 8, T, xn, dkeys, 0, self.ones)
        outs = []
        for b in range(4):
            outs.append(P.dma("sp", lambda e, b=b: e.dma_start(
                out=xn_o.rearrange("(c p) t -> p c t", p=128)[:, :, b * 512:(b + 1) * 512],
                in_=xn[:, :, b * 512:(b + 1) * 512]), [dkeys[b]], []))
        return outs

    def build_layer(self, A):
        P = self.P
        kind = self.kind
        self.mix_stores = []
        if kind == "tail":
            xT = self.din("xT", [D, T])
            self.mixT = self.din("mixT", [D, T], BF16)
            return self.tail(A, xT)
        gext = self.din("gext", [D, EXT], BF16)
        self.gext = gext
        if kind in ("even", "odd"):
            xT = self.din("xT", [D, T])
            self.mixT = self.dint("mixT", [D, T], BF16)
        else:
            self.mixT = self.dout("mixT", [D, T], BF16)
        self.identb = A.alloc([128, 128], BF16)
        ident_d = self.din("ident", [128, 128])
        self.load_w(self.identb, ident_d, "ident")
        m0 = A.mark()
        if kind.startswith("even"):
            self.even_mixer(A)
        else:
            self.odd_mixer(A)
        P.barrier()
        A.release(m0)
        if kind.endswith("_mix"):
            return self.mix_stores
        return self.tail(A, xT)

    def tail(self, A, xT):
        P = self.P
        ps = self.ps
        w_o = self.din("w_o", [D, D])
        w1 = self.din("w1", [D, 4 * D])
        w2 = self.din("w2", [4 * D, D])
        g2 = self.din("g2", [128, 8])
        gn = self.din("gnext", [128, 8])
        xo = self.dout("xoT", [D, T])
        xn_o = self.dout("xn_bf", [D, T], BF16)
        xn32_o = self.dout("xn_f32", [D, T])
        gains2 = A.alloc([128, 8], F32)
        gainsn = A.alloc([128, 8], F32)
        self.load(gains2, g2, "gains", q=getattr(self, "gq", "sp"))
        self.load(gainsn, gn, "gains", q=getattr(self, "gq", "sp"))
        xres = A.alloc([128, 8, T], F32)
        xkeys = ["xres%d" % b for b in range(4)]
        lvl = getattr(self, "tail_lvl", 9)
        for b in range(4 if lvl >= 1 else 0):
            self.load(xres[:, :, b * 512:(b + 1) * 512],
                      xT.rearrange("(c p) t -> p c t", p=128)[:, :, b * 512:(b + 1) * 512], xkeys[b])
        m1 = A.mark()
        mix = A.alloc([128, 8, T], BF16)
        for c in range(8 if lvl >= 2 else 0):
            self.load(mix[:, c, :], self.mixT[c * 128:(c + 1) * 128, :], "mix%d" % c)
        wo = A.alloc([128, 8, D], BF16)
        for h in range(2 if lvl >= 3 else 0):
            self.load_w(wo[:, :, h * 512:(h + 1) * 512],
                        w_o.rearrange("(kc p) n -> p kc n", p=128)[:, :, h * 512:(h + 1) * 512], "wo%d" % h)
        bank = 0
        for co in range(8 if not getattr(self, "no_wo_mm", False) else 0):
            for b in range(4):
                pk = "ps%d" % bank
                for kc in range(8):
                    P.pe(lambda e, co=co, b=b, kc=kc, bank=bank: e.matmul(
                        ps[:, bank, :], wo[:, kc, co * 128:(co + 1) * 128], mix[:, kc, b * 512:(b + 1) * 512],
                        start=(kc == 0), stop=(kc == 7)),
                        ["wo%d" % (co // 4), "mix%d" % kc], [pk])
                P.dve(lambda e, co=co, b=b, bank=bank: e.tensor_tensor(
                    out=xres[:, co, b * 512:(b + 1) * 512], in0=ps[:, bank, :],
                    in1=xres[:, co, b * 512:(b + 1) * 512], op=ALU.add), [pk, xkeys[b]], [xkeys[b]])
                bank = (bank + 1) % 4
        P.barrier()
        A.release(m1)
        if self.stop == "wo":
            return []
        xn2 = A.alloc([128, 8, T], BF16)
        nkeys = ["xn2_%d" % b for b in range(4)]
        self.rmsnorm_fm(A, xres, xkeys, gains2, 8, T, xn2, nkeys, 7, self.ones)
        hT = A.alloc([128, 4, T], BF16)
        w1b = [A.alloc([128, 8, 512], BF16) for _ in range(2)]
        w2b = [A.alloc([128, 4, D], BF16) for _ in range(2)]
        rl = [A.alloc([128, 512], F32) for _ in range(3)]
        NFB = 8
        bank = 0
        rli = 0
        for fb in range(NFB):
            i = fb % 2
            self.load_w(w1b[i], w1.rearrange("(kc p) n -> p kc n", p=128)[:, :, fb * 512:(fb + 1) * 512],
                        "w1b%d" % i)
            self.load_w(w2b[i], w2[fb * 512:(fb + 1) * 512, :].rearrange("(fc p) n -> p fc n", p=128),
                        "w2b%d" % i)
            for b in range(4):
                for fc in range(4):
                    pk = "ps%d" % bank
                    for kc in range(8):
                        P.pe(lambda e, i=i, fc=fc, b=b, kc=kc, bank=bank: e.matmul(
                            ps[:, bank, :], w1b[i][:, kc, fc * 128:(fc + 1) * 128],
                            xn2[:, kc, b * 512:(b + 1) * 512], start=(kc == 0), stop=(kc == 7)),
                            ["w1b%d" % i, nkeys[b]], [pk])
                    rk = "rl%d" % rli
                    P.act(lambda e, bank=bank, r=rl[rli]: e.activation(out=r, in_=ps[:, bank, :], func=AF.Relu),
                          [pk], [rk])
                    P.pool(lambda e, fc=fc, b=b, r=rl[rli]: e.tensor_tensor(
                        out=hT[:, fc, b * 512:(b + 1) * 512], in0=r, in1=r, op=ALU.mult),
                        [rk], ["hT%d_%d" % (fc, b)])
                    rli = (rli + 1) % 3
                    bank = (bank + 1) % 3
            for co in range(8):
                for b in range(4):
                    pk = "ps%d" % (4 + bank % 3)
                    bb = 4 + bank % 3
                    for fc in range(4):
                        P.pe(lambda e, i=i, fc=fc, b=b, co=co, bb=bb: e.matmul(
                            ps[:, bb, :], w2b[i][:, fc, co * 128:(co + 1) * 128],
                            hT[:, fc, b * 512:(b + 1) * 512], start=(fc == 0), stop=(fc == 3)),
                            ["w2b%d" % i, "hT%d_%d" % (fc, b)], [pk])
                    P.dve(lambda e, co=co, b=b, bb=bb: e.tensor_tensor(
                        out=xres[:, co, b * 512:(b + 1) * 512], in0=ps[:, bb, :],
                        in1=xres[:, co, b * 512:(b + 1) * 512], op=ALU.add), [pk, xkeys[b]], [xkeys[b]])
                    bank = (bank + 1) % 3
        P.barrier()
        A.release(m1)
        if self.stop == "mlp":
            return []
        xn = A.alloc([128, 8, T], BF16)
        xn32 = A.alloc([128, 8, 512], F32)
        outs = []
        for b in range(4):
            outs.append(P.dma("sp", lambda e, b=b: e.dma_start(
                out=xo.rearrange("(c p) t -> p c t", p=128)[:, :, b * 512:(b + 1) * 512],
                in_=xres[:, :, b * 512:(b + 1) * 512]), [xkeys[b]], []))
        dkeys = ["xnn%d" % b for b in range(4)]
        self.rmsnorm_fm(A, xres, xkeys, gainsn, 8, T, xn, dkeys, 7, self.ones)
        for b in range(4):
            outs.append(P.dma("sp", lambda e, b=b: e.dma_start(
                out=xn_o.rearrange("(c p) t -> p c t", p=128)[:, :, b * 512:(b + 1) * 512],
                in_=xn[:, :, b * 512:(b + 1) * 512]), [dkeys[b]], []))
        m2 = A.mark()
        sq = A.alloc([128, 8, 512], BF16)
        sr = A.alloc([128, 512], F32)
        rr = A.alloc([128, 512], F32)
        for b in range(4):
            sl = slice(b * 512, (b + 1) * 512)
            P.act(lambda e, sl=sl: e.activation(out=sq, in_=xres[:, :, sl], func=AF.Square), [xkeys[b]], ["fsq"])
            for c in range(8):
                P.pe(lambda e, c=c: e.matmul(ps[:, 6, :], self.ones[:, :], sq[:, c, :], start=(c == 0), stop=(c == 7)),
                     ["fsq", "ones"], ["ps6"])
            P.act(lambda e: e.activation(out=sr, in_=ps[:, 6, :], func=AF.Sqrt, bias=self.eps_t[:, 0:1],
                                         scale=1.0 / D), ["ps6"], ["fsr"])
            P.dve(lambda e: e.reciprocal(out=rr, in_=sr), ["fsr"], ["frr"])
            for c in range(8):
                P.dve(lambda e, c=c, sl=sl: e.scalar_tensor_tensor(
                    out=xn32[:, c, :], in0=xres[:, c, sl], scalar=gainsn[:, c:c + 1], in1=rr,
                    op0=ALU.mult, op1=ALU.mult), [xkeys[b], "frr", "gains"], ["xn32"])
            outs.append(P.dma("sp", lambda e, sl=sl: e.dma_start(
                out=xn32_o.rearrange("(c p) t -> p c t", p=128)[:, :, sl], in_=xn32), ["xn32"], []))
        A.release(m2)
        return outs

    def rope_block(self, A, bufs, psrc, pskey, n, rot_lhsT, rot_key, Ct, St, tkeys, dst, dkey, prow, pr_bank,
                   extra=()):
        P = self.P
        ps = self.ps
        qb, t1, t2, kq, k1, k2 = bufs
        P.act(lambda e: e.activation(out=qb[prow, 0:n], in_=psrc, func=AF.Copy), [pskey] + list(extra), [kq])
        prk = "ps%d" % pr_bank
        P.pe(lambda e: e.matmul(ps[prow, pr_bank, 0:n], rot_lhsT, qb[prow, 0:n], start=True, stop=True),
             [kq, rot_key], [prk])
        P.dve(lambda e: e.tensor_tensor(out=t1[prow, 0:n], in0=psrc, in1=Ct, op=ALU.mult),
              [pskey, kq] + tkeys, [k1])
        P.dve(lambda e: e.tensor_tensor(out=t2[prow, 0:n], in0=ps[prow, pr_bank, 0:n], in1=St, op=ALU.mult),
              [prk] + tkeys, [k2])
        P.pool(lambda e: e.tensor_tensor(out=dst, in0=t1[prow, 0:n], in1=t2[prow, 0:n], op=ALU.add),
               [k1, k2], [dkey])

    def rope_bufs(self, A, nsets=2):
        sets = []
        for _ in range(nsets):
            sets.append((A.alloc([128, 512], BF16), A.alloc([128, 512], F32), A.alloc([128, 512], F32),
                         self.key("rqb"), self.key("rt1"), self.key("rt2")))
        return sets

    def odd_mixer(self, A):
        P = self.P
        ps = self.ps
        gext = self.gext
        w_in = self.din("w_in", [D, 9216])
        ropeC = self.din("ropeC", [128, EXT])
        ropeS = self.din("ropeS", [128, EXT])
        rot_d = self.din("rot64", [128, 128])
        vb_d = self.din("vbias", [128, 72])
        masks_d = self.din("masks", [128, 512])
        gv = gext.rearrange("(c p) t -> p c t", p=128)
        xn = A.alloc([128, 8, T], BF16)
        for b in range(4):
            self.load(xn[:, :, b * 512:(b + 1) * 512], gv[:, :, HALO + b * 512:HALO + (b + 1) * 512], "xn%d" % b)
        Ct = A.alloc([128, EXT], F32)
        St = A.alloc([128, EXT], F32)
        self.load(Ct, ropeC, "ropeT", q="sp")
        self.load(St, ropeS, "ropeT", q="sp")
        rot = A.alloc([128, 128], BF16)
        self.load_w(rot, rot_d, "rot")
        vb = A.alloc([128, 72], F32)
        self.load(vb, vb_d, "vb")
        masks = A.alloc([128, 512], BF16)
        self.load_w(masks, masks_d, "masks")
        xh = [A.alloc([128, 8, 512], BF16) for _ in range(2)]
        wq = [A.alloc([128, 8, 128], BF16) for _ in range(2)]
        wk = [A.alloc([128, 8, 128], BF16) for _ in range(2)]
        wv = [A.alloc([128, 8, 128], BF16) for _ in range(2)]
        QTm = [A.alloc([128, T], BF16) for _ in range(2)]
        P.pool(lambda e: e.memset(QTm[0][64:128, :], 0.0), [], ["QT"])
        P.pool(lambda e: e.memset(QTm[1][0:64, :], 0.0), [], ["QT"])
        KT = A.alloc([128, EXT + 128], BF16)
        VT = A.alloc([128, EXT + 128], BF16)
        Vt = A.alloc([128, 33, 128], BF16)
        acc = A.alloc([128, T], F32)
        zacc = A.alloc([128, T], F32)
        rz = A.alloc([128, T], F32)
        mixc = [A.alloc([128, T], BF16) for _ in range(2)]
        PT = [A.alloc([128, 256], BF16) for _ in range(3)]
        rb = self.rope_bufs(A, 2)
        wv_w = w_in.rearrange("(kc p) n -> p kc n", p=128)
        if self.stop == "o0":
            return
        it = 0
        xhi = 0
        rbi = 0
        pti = 0
        proj_bank = 0
        s_bank = 0
        o_bank = 0
        vtile_base = [0, 0, 0]
        for c in range(getattr(self, "nchunks", 8)):
            for g in getattr(self, "gorder", (0, 1, 2)):
                w, d = DIL[g]
                halo = 64 * d
                L = T // d + 128
                npos = T // d
                wi = it % 2
                it += 1
                col = g * 3072 + c * 128
                self.load_w(wq[wi], wv_w[:, :, col:col + 128], "wq%d" % wi)
                self.load_w(wk[wi], wv_w[:, :, col + 1024:col + 1024 + 128], "wk%d" % wi)
                self.load_w(wv[wi], wv_w[:, :, col + 2048:col + 2048 + 128], "wv%d" % wi)
                KTv = KT[:, 0:d * L].rearrange("p (dd m) -> p dd m", dd=d)
                VTv = VT[:, 0:d * L].rearrange("p (dd m) -> p dd m", dd=d)
                QTv = [QTm[hh][:, :].rearrange("p (dd m) -> p dd m", dd=d) for hh in range(2)]
                blocks = []
                hb = min(halo, 512)
                for u0 in range(0, halo, hb):
                    blocks.append(("halo", u0, hb, HALO - halo + u0))
                for b in range(4):
                    blocks.append(("own", halo + b * 512, 512, b))
                for u0 in range(0, halo, hb):
                    blocks.append(("halo", halo + T + u0, hb, HALO + T + u0))
                for (src, u0, n, info) in blocks:
                    if src == "halo":
                        xi = xhi % 2
                        xhi += 1
                        xk = "xh%d" % xi
                        self.load(xh[xi][:, :, 0:n], gv[:, :, info:info + n], xk)
                        xs = xh[xi][:, :, 0:n]
                        xkeys = [xk]
                        tpos = info
                    else:
                        xs = xn[:, :, info * 512:(info + 1) * 512]
                        xkeys = ["xn%d" % info]
                        tpos = HALO + info * 512
                    m0, m1 = u0 // d, (u0 + n) // d
                    todo = [("k", wk[wi], "wk%d" % wi), ("v", wv[wi], "wv%d" % wi)]
                    if src == "own":
                        todo.append(("q", wq[wi], "wq%d" % wi))
                    for (what, wt, wkey) in todo:
                        pk = "ps%d" % proj_bank
                        pb = proj_bank
                        proj_bank ^= 1
                        for kc in range(8):
                            P.pe(lambda e, wt=wt, kc=kc, xs=xs, n=n, pb=pb: e.matmul(
                                ps[:, pb, 0:n], wt[:, kc, :], xs[:, kc, :], start=(kc == 0), stop=(kc == 7)),
                                [wkey] + xkeys, [pk])
                        src_v = ps[:, pb, 0:n].rearrange("p (m dd) -> p dd m", dd=d)
                        dbg = getattr(self, "dbg", "")
                        if what == "v" or ("norope" in dbg and what == "k"):
                            if "novevac" not in dbg:
                                self.evac(VTv[:, :, m0:m1] if what == "v" else KTv[:, :, m0:m1], src_v, [pk],
                                          ["VT" if what == "v" else "KT"])
                        elif "norope" in dbg:
                            pass
                        else:
                            if what == "k":
                                dst = KTv[:, :, m0:m1]
                                dkey = "KT"
                            else:
                                q0 = (u0 - halo) // d
                                dst = None
                                dkey = "QT"
                            bufs = rb[rbi % 2]
                            rbi += 1
                            qb, t1, t2, kq, k1, k2 = bufs
                            P.act(lambda e, qb=qb, n=n, pb=pb: e.activation(out=qb[:, 0:n], in_=ps[:, pb, 0:n],
                                                                           func=AF.Copy), [pk], [kq])
                            P.pe(lambda e, qb=qb, n=n: e.matmul(ps[:, 2, 0:n], rot[:, :], qb[:, 0:n],
                                                               start=True, stop=True), [kq, "rot"], ["ps2"])
                            P.dve(lambda e, t1=t1, n=n, pb=pb, tpos=tpos: e.tensor_tensor(
                                out=t1[:, 0:n], in0=ps[:, pb, 0:n], in1=Ct[:, tpos:tpos + n], op=ALU.mult),
                                [pk, "ropeT", kq], [k1])
                            P.dve(lambda e, t2=t2, n=n, tpos=tpos: e.tensor_tensor(
                                out=t2[:, 0:n], in0=ps[:, 2, 0:n], in1=St[:, tpos:tpos + n], op=ALU.mult),
                                ["ps2", "ropeT"], [k2])
                            P.dve(lambda e, t1=t1, t2=t2, n=n: e.tensor_tensor(
                                out=t1[:, 0:n], in0=t1[:, 0:n], in1=t2[:, 0:n], op=ALU.add), [k1, k2], [k1])
                            if dst is not None:
                                P.act(lambda e, dst=dst, t1=t1, n=n, d=d: e.activation(
                                    out=dst, in_=t1[:, 0:n].rearrange("p (m dd) -> p dd m", dd=d), func=AF.Copy),
                                    [k1], [dkey])
                            else:
                                for hh in range(2):
                                    hs = slice(hh * 64, (hh + 1) * 64)
                                    P.act(lambda e, hs=hs, hh=hh, QTv=QTv, q0=q0, t1=t1, n=n, d=d: e.activation(
                                        out=QTv[hh][hs, :, q0:q0 + n // d],
                                        in_=t1[hs, 0:n].rearrange("p (m dd) -> p dd m", dd=d), func=AF.Copy),
                                        [k1], [dkey])
                if self.stop == "o1":
                    return
                ntile_ph = L // 128
                for p in range(d):
                    for i in range(ntile_ph):
                        ti = p * ntile_ph + i
                        pst = ps[:, 7, :].bitcast(BF16)
                        P.pe(lambda e, p=p, i=i, pst=pst, VTv=VTv: e.transpose(pst[:, 0:128], VTv[:, p, i * 128:(i + 1) * 128],
                                                                      self.identb[:, :]), ["VT", "ident"], ["ps7"])
                        self.evac(Vt[:, ti, :], pst[:, 0:128], ["ps7"], ["Vt"], which="dve")
                if self.stop == "o2":
                    return
                for p in range(d):
                    for qi in range(npos // 128):
                        ob = 5 + (o_bank % 2)
                        o_bank += 1
                        ok = "ps%d" % ob
                        for blk in range(2):
                            kti = qi + blk
                            ti = p * ntile_ph + kti
                            sb = 3 + (s_bank % 2)
                            s_bank += 1
                            sk = "ps%d" % sb
                            for hh in range(2):
                                hs = slice(hh * 64, (hh + 1) * 64)
                                P.pe(lambda e, hs=hs, hh=hh, p=p, kti=kti, qi=qi, sb=sb, KTv=KTv, QTv=QTv: e.matmul(
                                    ps[:, sb, hh * 128:(hh + 1) * 128], KTv[:, p, kti * 128:(kti + 1) * 128],
                                    QTv[hh][:, p, qi * 128:(qi + 1) * 128], start=True, stop=True),
                                    ["KT", "QT"], [sk])
                            pt = PT[pti % 3]
                            ptk = "PT%d" % (pti % 3)
                            pti += 1
                            vcol = vtile_col(g, ti)
                            P.act(lambda e, pt=pt, sb=sb, vcol=vcol: e.activation(
                                out=pt, in_=ps[:, sb, 0:256], func=AF.Exp, bias=vb[:, vcol:vcol + 1], scale=0.125),
                                [sk, "vb"], [ptk])
                            P.dve(lambda e, pt=pt, blk=blk: e.tensor_tensor(
                                out=pt, in0=pt, in1=masks[:, blk * 256:(blk + 1) * 256], op=ALU.mult),
                                [ptk, "masks"], [ptk])
                            P.pe(lambda e, pt=pt, ob=ob, blk=blk: e.matmul(
                                ps[:, ob, 128:384], self.ones[:, :], pt, start=(blk == 0), stop=(blk == 1),
                                skip_group_check=True), [ptk, "ones"], [ok])
                            for hh in range(2):
                                hs = slice(hh * 64, (hh + 1) * 64)
                                P.pe(lambda e, pt=pt, ob=ob, hh=hh, hs=hs, ti=ti, blk=blk: e.matmul(
                                    ps[hs, ob, 0:128], Vt[:, ti, hs], pt[:, hh * 128:(hh + 1) * 128],
                                    start=False, stop=(blk == 1), skip_group_check=True), [ptk, "Vt"], [ok])
                        accv = acc[:, :].rearrange("p (m dd) -> p dd m", dd=d)[:, p, qi * 128:(qi + 1) * 128]
                        zv = zacc[:, :].rearrange("p (m dd) -> p dd m", dd=d)[:, p, qi * 128:(qi + 1) * 128]
                        if g == 0:
                            P.dve(lambda e, accv=accv, ob=ob: e.tensor_copy(out=accv, in_=ps[:, ob, 0:128]),
                                  [ok], ["acc"])
                            P.dve(lambda e, zv=zv, ob=ob: e.tensor_copy(out=zv[0:64], in_=ps[0:64, ob, 128:256]),
                                  [ok], ["zacc"])
                            P.dve(lambda e, zv=zv, ob=ob: e.tensor_copy(out=zv[64:128], in_=ps[64:128, ob, 256:384]),
                                  [ok], ["zacc"])
                        else:
                            P.dve(lambda e, accv=accv, ob=ob: e.tensor_tensor(
                                out=accv, in0=ps[:, ob, 0:128], in1=accv, op=ALU.add), [ok, "acc"], ["acc"])
                            P.dve(lambda e, zv=zv, ob=ob: e.tensor_tensor(
                                out=zv[0:64], in0=ps[0:64, ob, 128:256], in1=zv[0:64], op=ALU.add),
                                [ok, "zacc"], ["zacc"])
                            P.dve(lambda e, zv=zv, ob=ob: e.tensor_tensor(
                                out=zv[64:128], in0=ps[64:128, ob, 256:384], in1=zv[64:128], op=ALU.add),
                                [ok, "zacc"], ["zacc"])
            mi = c % 2
            for b in range(4):
                sl = slice(b * 512, (b + 1) * 512)
                P.dve(lambda e, sl=sl: e.reciprocal(out=rz[:, sl], in_=zacc[:, sl]), ["zacc"], ["rz%d" % b])
                P.pool(lambda e, sl=sl, mi=mi: e.tensor_tensor(out=mixc[mi][:, sl], in0=acc[:, sl], in1=rz[:, sl],
                                                              op=ALU.mult), ["acc", "rz%d" % b], ["mixc%d" % mi])
            self.mix_stores.append(P.dma("sp", lambda e, c=c, mi=mi: e.dma_start(
                out=self.mixT[c * 128:(c + 1) * 128, :], in_=mixc[mi]), ["mixc%d" % mi], ["mixT"]))

    def even_mixer(self, A):
        P = self.P
        ps = self.ps
        gext = self.gext
        gall = self.din("gall", [D, S], BF16)
        w_in = self.din("w_in", [D, 1952])
        w_uq = self.din("w_uq", [256, 768])
        w_ukv = self.din("w_ukv", [128, 1024])
        qn_d = self.din("qn", [128, 2])
        kvn_d = self.din("kvn", [128, 1])
        nab_d = self.din("nab", [5, 4, 128, 6, 256])
        rkC = self.din("rkC", [32, S])
        rkS = self.din("rkS", [32, S])
        rqC = self.din("rqC", [32, T])
        rqS = self.din("rqS", [32, T])
        rot_d = self.din("rot32", [32, 32])
        gv = gext.rearrange("(c p) t -> p c t", p=128)
        gav = gall.rearrange("(c p) t -> p c t", p=128)
        wv_w = w_in.rearrange("(kc p) n -> p kc n", p=128)
        R = slice(64, 96)
        cqn = A.alloc([128, 2, T], BF16)
        gq = A.alloc([128, 2], F32)
        gkv = A.alloc([128, 1], F32)
        self.load(gq, qn_d, "gains")
        self.load(gkv, kvn_d, "gains")
        rot = A.alloc([128, 32], BF16)
        self.load_w(rot[R, :], rot_d, "rot")
        m_na = A.mark()
        xn = A.alloc([128, 8, T], BF16)
        for b in range(4):
            self.load(xn[:, :, b * 512:(b + 1) * 512], gv[:, :, HALO + b * 512:HALO + (b + 1) * 512], "xn%d" % b)
        xh = A.alloc([128, 8, 512], BF16)
        self.load(xh[:, :, 0:256], gv[:, :, HALO - 256:HALO], "xh")
        self.load(xh[:, :, 256:512], gv[:, :, HALO + T:HALO + T + 256], "xh")
        wql = A.alloc([128, 8, 256], BF16)
        self.load_w(wql, wv_w[:, :, 1536:1792], "wql")
        cq = A.alloc([128, 2, T], F32)
        pbk = 0
        for b in range(4):
            for cc in range(2):
                pk = "ps%d" % pbk
                for kc in range(8):
                    P.pe(lambda e, cc=cc, kc=kc, b=b, pbk=pbk: e.matmul(
                        ps[:, pbk, :], wql[:, kc, cc * 128:(cc + 1) * 128], xn[:, kc, b * 512:(b + 1) * 512],
                        start=(kc == 0), stop=(kc == 7)), ["wql", "xn%d" % b], [pk])
                self.evac(cq[:, cc, b * 512:(b + 1) * 512], ps[:, pbk, :], [pk], ["cq%d" % b])
                pbk ^= 1
        self.rmsnorm_fm(A, cq, ["cq%d" % b for b in range(4)], gq, 2, T, cqn, ["cqn%d" % b for b in range(4)],
                        7, self.ones)
        if self.stop == "na0":
            return
        wq = [A.alloc([128, 8, 128], BF16) for _ in range(2)]
        wk = [A.alloc([128, 8, 128], BF16) for _ in range(2)]
        wvv = [A.alloc([128, 8, 128], BF16) for _ in range(2)]
        QTm = [A.alloc([128, T], BF16) for _ in range(2)]
        P.pool(lambda e: e.memset(QTm[0][64:128, :], 0.0), [], ["QT"])
        P.pool(lambda e: e.memset(QTm[1][0:64, :], 0.0), [], ["QT"])
        NK = T + 512
        KT = A.alloc([128, NK], BF16)
        VT = A.alloc([128, NK], BF16)
        Vt = A.alloc([128, NK // 128, 128], BF16)
        nab = [A.alloc([128, 6, 256], F32) for _ in range(2)]
        sbuf_s = [A.alloc([128, 256], F32) for _ in range(2)]
        PT = [A.alloc([128, 256], BF16) for _ in range(3)]
        rzn = [A.alloc([128, 128], F32) for _ in range(2)]
        mixc = [A.alloc([128, T], BF16) for _ in range(2)]
        proj_bank = 0
        s_bank = 0
        o_bank = 0
        pti = 0
        nbi = 0
        for c in range(4):
            wi = c % 2
            col = c * 128
            self.load_w(wq[wi], wv_w[:, :, col:col + 128], "wq%d" % wi)
            self.load_w(wk[wi], wv_w[:, :, 512 + col:512 + col + 128], "wk%d" % wi)
            self.load_w(wvv[wi], wv_w[:, :, 1024 + col:1024 + col + 128], "wv%d" % wi)
            blocks = [("halo", 0, 256, xh[:, :, 0:256], ["xh"])]
            for b in range(4):
                blocks.append(("own", 256 + b * 512, 512, xn[:, :, b * 512:(b + 1) * 512], ["xn%d" % b]))
            blocks.append(("halo", 256 + T, 256, xh[:, :, 256:512], ["xh"]))
            for (src, u0, n, xs, xkeys) in blocks:
                todo = [(wk[wi], "wk%d" % wi, KT[:, u0:u0 + n], "KT"), (wvv[wi], "wv%d" % wi, VT[:, u0:u0 + n], "VT")]
                if src == "own":
                    todo.append((wq[wi], "wq%d" % wi, None, "QT"))
                for (wt, wkey, dst, dkey) in todo:
                    pb = proj_bank
                    proj_bank ^= 1
                    pk = "ps%d" % pb
                    for kc in range(8):
                        P.pe(lambda e, wt=wt, kc=kc, xs=xs, n=n, pb=pb: e.matmul(
                            ps[:, pb, 0:n], wt[:, kc, :], xs[:, kc, :], start=(kc == 0), stop=(kc == 7)),
                            [wkey] + xkeys, [pk])
                    if dst is not None:
                        self.evac(dst, ps[:, pb, 0:n], [pk], [dkey])
                    else:
                        self.evac(QTm[0][0:64, u0 - 256:u0 - 256 + n], ps[0:64, pb, 0:n], [pk], [dkey], which="dve")
                        self.evac(QTm[1][64:128, u0 - 256:u0 - 256 + n], ps[64:128, pb, 0:n], [pk], [dkey], which="dve")
            if self.stop == "na1":
                return
            for ti in range(NK // 128):
                pst = ps[:, 7, :].bitcast(BF16)
                P.pe(lambda e, ti=ti, pst=pst: e.transpose(pst[:, 0:128], VT[:, ti * 128:(ti + 1) * 128],
                                                          self.identb[:, :]), ["VT", "ident"], ["ps7"])
                self.evac(Vt[:, ti, :], pst[:, 0:128], ["ps7"], ["Vt"], which="dve")
            if self.stop == "na2":
                return
            for qi in range(16):
                offs = na_offsets(qi)
                slot = na_slot(qi)
                nb = nab[nbi % 2]
                nbk = "nab%d" % (nbi % 2)
                nbi += 1
                self.load(nb, nab_d[slot, c], nbk, q="sp")
                ob = 5 + (o_bank % 2)
                o_bank += 1
                ok = "ps%d" % ob
                for j, off in enumerate(offs):
                    kt = qi + off + 2
                    sb = 3 + (s_bank % 2)
                    ssb = sbuf_s[s_bank % 2]
                    ssk = "ssb%d" % (s_bank % 2)
                    s_bank += 1
                    sk = "ps%d" % sb
                    for hh in range(2):
                        hs = slice(hh * 64, (hh + 1) * 64)
                        P.pe(lambda e, hs=hs, hh=hh, kt=kt, qi=qi, sb=sb: e.matmul(
                            ps[:, sb, hh * 128:(hh + 1) * 128], KT[:, kt * 128:(kt + 1) * 128],
                            QTm[hh][:, qi * 128:(qi + 1) * 128], start=True, stop=True), ["KT", "QT"], [sk])
                    P.dve(lambda e, ssb=ssb, sb=sb, nb=nb, j=j: e.scalar_tensor_tensor(
                        out=ssb, in0=ps[:, sb, 0:256], scalar=0.125, in1=nb[:, j, :], op0=ALU.mult, op1=ALU.add),
                        [sk, nbk], [ssk])
                    pt = PT[pti % 3]
                    ptk = "PT%d" % (pti % 3)
                    pti += 1
                    P.act(lambda e, pt=pt, ssb=ssb: e.activation(out=pt, in_=ssb, func=AF.Exp), [ssk], [ptk])
                    last = (j == len(offs) - 1)
                    dbg = getattr(self, "dbg", "")
                    if "noones" not in dbg:
                        P.pe(lambda e, pt=pt, ob=ob, j=j, last=last: e.matmul(
                            ps[:, ob, 128:384], self.ones[:, :], pt, start=(j == 0), stop=last,
                            skip_group_check=True), [ptk, "ones"], [ok])
                    if "nopv" not in dbg:
                        for hh in range(2):
                            hs = slice(hh * 64, (hh + 1) * 64)
                            P.pe(lambda e, pt=pt, ob=ob, hh=hh, hs=hs, kt=kt, last=last, j=j, dbg=dbg: e.matmul(
                                ps[hs, ob, 0:128], Vt[:, kt, hs], pt[:, hh * 128:(hh + 1) * 128],
                                start=(j == 0 and "noones" in dbg), stop=last, skip_group_check=True), [ptk, "Vt"], [ok])
                rzz = rzn[qi % 2]
                rzk = "rzn%d" % (qi % 2)
                if "norecip" in dbg:
                    P.dve(lambda e, rzz=rzz, ob=ob, qi=qi, wi=wi: e.tensor_copy(
                        out=mixc[wi][:, qi * 128:(qi + 1) * 128], in_=ps[:, ob, 0:128]), [ok], ["mixc%d" % wi])
                    continue
                P.dve(lambda e, rzz=rzz, ob=ob: e.reciprocal(out=rzz[0:64, :], in_=ps[0:64, ob, 128:256]), [ok], [rzk])
                P.dve(lambda e, rzz=rzz, ob=ob: e.reciprocal(out=rzz[64:128, :], in_=ps[64:128, ob, 256:384]),
                      [ok], [rzk])
                P.dve(lambda e, rzz=rzz, ob=ob, qi=qi, wi=wi: e.tensor_tensor(
                    out=mixc[wi][:, qi * 128:(qi + 1) * 128], in0=ps[:, ob, 0:128], in1=rzz, op=ALU.mult),
                    [ok, rzk], ["mixc%d" % wi])
            self.mix_stores.append(P.dma("sp", lambda e, c=c, wi=wi: e.dma_start(
                out=self.mixT[c * 128:(c + 1) * 128, :], in_=mixc[wi]), ["mixc%d" % wi], ["mixT"]))
        P.barrier()
        A.release(m_na)
        if self.stop == "na":
            return
        wl = A.alloc([128, 8, 160], BF16)
        self.load_w(wl, wv_w[:, :, 1792:1952], "wl")
        wuq = A.alloc([128, 2, 768], BF16)
        self.load_w(wuq, w_uq.rearrange("(kc p) n -> p kc n", p=128), "wuq")
        wukv = A.alloc([128, 1024], BF16)
        self.load_w(wukv, w_ukv, "wukv")
        ckvn = A.alloc([128, S], BF16)
        KTm = A.alloc([128, S], BF16)
        P.pool(lambda e: e.memset(KTm[64:128, :], 0.0), [], ["KTpe%d" % kb for kb in range(16)])
        Cq = A.alloc([128, T], F32)
        Sq = A.alloc([128, T], F32)
        self.load(Cq[R, :], rqC, "ropeq", q="sp")
        self.load(Sq[R, :], rqS, "ropeq", q="sp")
        m_lat = A.mark()
        gb = [A.alloc([128, 8, 512], BF16) for _ in range(2)]
        Ck = [A.alloc([128, 512], F32) for _ in range(2)]
        Sk = [A.alloc([128, 512], F32) for _ in range(2)]
        sq = [A.alloc([128, 512], BF16) for _ in range(2)]
        sr = [A.alloc([128, 512], F32) for _ in range(2)]
        rr = [A.alloc([128, 512], F32) for _ in range(2)]
        rb = self.rope_bufs(A, 2)
        for kb in range(16):
            i = kb % 2
            sl = slice(kb * 512, (kb + 1) * 512)
            self.load(gb[i], gav[:, :, sl], "gb%d" % i)
            self.load(Ck[i][R, :], rkC[:, sl], "rk%d" % i, q="sp")
            self.load(Sk[i][R, :], rkS[:, sl], "rk%d" % i, q="sp")
            pc = i
            pck = "ps%d" % pc
            for kc in range(8):
                P.pe(lambda e, kc=kc, i=i, pc=pc: e.matmul(ps[:, pc, :], wl[:, kc, 0:128], gb[i][:, kc, :],
                                                          start=(kc == 0), stop=(kc == 7)), ["wl", "gb%d" % i], [pck])
            for kc in range(8):
                P.pe(lambda e, kc=kc, i=i: e.matmul(ps[R, 2, :], wl[:, kc, 128:160], gb[i][:, kc, :],
                                                   start=(kc == 0), stop=(kc == 7)), ["wl", "gb%d" % i], ["ps2"])
            P.act(lambda e, i=i, pc=pc: e.activation(out=sq[i], in_=ps[:, pc, :], func=AF.Square), [pck], ["lsq%d" % i])
            P.pe(lambda e, i=i: e.matmul(ps[:, 3, :], self.ones[:, :], sq[i], start=True, stop=True),
                 ["lsq%d" % i, "ones"], ["ps3"])
            P.act(lambda e, i=i: e.activation(out=sr[i], in_=ps[:, 3, :], func=AF.Sqrt, bias=self.eps_t[:, 0:1],
                                              scale=1.0 / 128), ["ps3"], ["lsr%d" % i])
            P.dve(lambda e, i=i: e.reciprocal(out=rr[i], in_=sr[i]), ["lsr%d" % i], ["lrr%d" % i])
            P.dve(lambda e, i=i, pc=pc, sl=sl: e.scalar_tensor_tensor(
                out=ckvn[:, sl], in0=ps[:, pc, :], scalar=gkv[:, 0:1], in1=rr[i], op0=ALU.mult, op1=ALU.mult),
                [pck, "lrr%d" % i, "gains"], ["ckvn%d" % kb])
            self.rope_block(A, rb[i], ps[R, 2, :], "ps2", 512, rot[R, :], "rot", Ck[i][R, :], Sk[i][R, :],
                            ["rk%d" % i], KTm[R, sl], "KTpe%d" % kb, R, 4)
        P.barrier()
        A.release(m_lat)
        if self.stop == "lat":
            return
        Vh = [A.alloc([128, 64, 128], BF16) for _ in range(2)]
        P.pool(lambda e: e.memset(Vh[0][:, :, 64:128], 1.0), [], ["Vh0"])
        P.pool(lambda e: e.memset(Vh[1][:, :, 0:64], 1.0), [], ["Vh1"])
        QTh = [A.alloc([128, T], BF16) for _ in range(2)]
        P.pool(lambda e: e.memset(QTh[0][64:128, :], 0.0), [], ["QTh0"])
        P.pool(lambda e: e.memset(QTh[1][64:128, :], 0.0), [], ["QTh1"])
        PTm = [A.alloc([128, 512], BF16) for _ in range(3)]
        rzm = [A.alloc([128, 512], F32) for _ in range(2)]
        mixh = [A.alloc([128, T], BF16) for _ in range(2)]
        rb = self.rope_bufs(A, 2)
        scale = 1.0 / math.sqrt(96.0)
        pti = 0
        s_bank = 0
        rbi = 0
        for h in range(8):
            par = h % 2
            vsl = slice(0, 64) if par == 0 else slice(64, 128)
            zsl = slice(64, 128) if par == 0 else slice(0, 64)
            for kb in range(16):
                sl = slice(kb * 512, (kb + 1) * 512)
                pb = 5 + (kb % 2)
                pk = "ps%d" % pb
                P.pe(lambda e, h=h, sl=sl, pb=pb: e.matmul(ps[0:64, pb, :], wukv[:, h * 128:h * 128 + 64], ckvn[:, sl],
                                                          start=True, stop=True), ["wukv", "ckvn%d" % kb], [pk])
                self.evac(KTm[0:64, sl], ps[0:64, pb, :], [pk], ["KTm"], which="dve")
            if self.stop == "h0":
                return
            for k8 in range(8):
                pb = 5 + (k8 % 2)
                pk = "ps%d" % pb
                for j in range(8):
                    kt = k8 * 8 + j
                    P.pe(lambda e, h=h, kt=kt, j=j, pb=pb: e.matmul(
                        ps[:, pb, j * 64:(j + 1) * 64], ckvn[:, kt * 128:(kt + 1) * 128],
                        wukv[:, h * 128 + 64:h * 128 + 128], start=True, stop=True),
                        ["wukv", "ckvn%d" % (kt // 4)], [pk])
                P.dve(lambda e, k8=k8, pb=pb, par=par, vsl=vsl: e.tensor_copy(
                    out=Vh[par][:, k8 * 8:(k8 + 1) * 8, vsl],
                    in_=ps[:, pb, :].rearrange("p (j v) -> p j v", j=8)), [pk], ["Vh%d" % par])
            if self.stop == "h1":
                return
            qt = QTh[par]
            qk = "QTh%d" % par
            for b in range(4):
                sl = slice(b * 512, (b + 1) * 512)
                pb = 5 + (b % 2)
                pk = "ps%d" % pb
                for kc in range(2):
                    P.pe(lambda e, h=h, kc=kc, sl=sl, pb=pb: e.matmul(
                        ps[0:96, pb, :], wuq[:, kc, h * 96:h * 96 + 96], cqn[:, kc, sl], start=(kc == 0),
                        stop=(kc == 1)), ["wuq", "cqn%d" % b], [pk])
                self.evac(qt[0:64, sl], ps[0:64, pb, :], [pk], [qk], which="dve")
                self.rope_block(A, rb[rbi % 2], ps[R, pb, :], pk, 512, rot[R, :], "rot", Cq[R, sl], Sq[R, sl],
                                ["ropeq"], qt[R, sl], qk, R, 7, extra=[qk])
                rbi += 1
            if self.stop == "h2":
                return
            kpe_keys = ["KTpe%d" % kb for kb in range(16)]
            for qb in range(4):
                qsl = slice(qb * 512, (qb + 1) * 512)
                ob = 3 + (qb % 2)
                ok = "ps%d" % ob
                for kt in range(64):
                    sb = s_bank % 3
                    s_bank += 1
                    sk = "ps%d" % sb
                    P.pe(lambda e, kt=kt, qsl=qsl, sb=sb, qt=qt: e.matmul(
                        ps[:, sb, :], KTm[:, kt * 128:(kt + 1) * 128], qt[:, qsl], start=True, stop=True),
                        ["KTm", qk, kpe_keys[kt // 4]], [sk])
                    pt = PTm[pti % 3]
                    ptk = "PTm%d" % (pti % 3)
                    pti += 1
                    P.act(lambda e, pt=pt, sb=sb: e.activation(out=pt, in_=ps[:, sb, :], func=AF.Exp, scale=scale),
                          [sk], [ptk])
                    P.pe(lambda e, pt=pt, kt=kt, ob=ob, par=par: e.matmul(
                        ps[:, ob, :], Vh[par][:, kt, :], pt, start=(kt == 0), stop=(kt == 63)),
                        [ptk, "Vh%d" % par], [ok])
                if self.stop == "h3":
                    return
                rzz = rzm[qb % 2]
                rzk = "rzm%d" % (qb % 2)
                P.dve(lambda e, rzz=rzz, ob=ob, vsl=vsl, zsl=zsl: e.reciprocal(out=rzz[vsl, :], in_=ps[zsl, ob, :]),
                      [ok], [rzk])
                mh = mixh[(h // 2) % 2]
                mk = "mixh%d" % ((h // 2) % 2)
                P.dve(lambda e, rzz=rzz, ob=ob, vsl=vsl, qsl=qsl, mh=mh: e.tensor_tensor(
                    out=mh[vsl, qsl], in0=ps[vsl, ob, :], in1=rzz[vsl, :], op=ALU.mult), [ok, rzk], [mk])
            if par == 1:
                cch = 4 + h // 2
                self.mix_stores.append(P.dma("sp", lambda e, cch=cch, mh=mh: e.dma_start(
                    out=self.mixT[cch * 128:(cch + 1) * 128, :], in_=mh), [mk], ["mixT"]))


def na_offsets(qi):
    if qi == 0:
        return [-2, -1, 0, 1, 2, 3]
    if qi == 15:
        return [-3, -2, -1, 0, 1, 2]
    return [-2, -1, 0, 1, 2]


def na_slot(qi):
    return {0: 0, 1: 1, 14: 2, 15: 3}.get(qi, 4)


def vtile_col(g, ti):
    return (0, 17, 37)[g] + ti


def pc_layout(v, C):
    return np.ascontiguousarray(np.asarray(v, np.float32).reshape(C, 128).T)


def rope_table_np(npos_start, n, dim, rows):
    half = dim // 2
    inv = (1.0 / (10000.0 ** (np.arange(0, dim, 2, dtype=np.float32) / np.float32(dim)))).astype(np.float32)
    pos = np.arange(npos_start, npos_start + n, dtype=np.float32)
    pos = np.clip(pos, 0, S - 1)
    ang = (pos[None, :] * inv[:, None]).astype(np.float32)
    idx = np.arange(rows) % half
    return np.cos(ang)[idx].astype(np.float32), np.sin(ang)[idx].astype(np.float32)


def rot_matrix(nheads, dh):
    n = nheads * dh
    h = dh // 2
    M = np.zeros((n, n), np.float32)
    for b in range(nheads):
        for i in range(dh):
            if i < h:
                M[b * dh + i + h, b * dh + i] = -1.0
            else:
                M[b * dh + i - h, b * dh + i] = 1.0
    return M


def dil_masks():
    a = np.arange(128)[:, None]
    b = np.arange(128)[None, :]
    mA = (a >= b).astype(np.float32)
    mB = (a <= b).astype(np.float32)
    return np.ascontiguousarray(np.concatenate([mA, mA, mB, mB], axis=1))


def dil_vbias(j):
    vb = np.zeros((128, 72), np.float32)
    own0 = j * T
    for g, (w, d) in enumerate(DIL):
        L = T // d + 128
        ntile_ph = L // 128
        for p in range(d):
            for i in range(ntile_ph):
                m = i * 128 + np.arange(128)
                tok = own0 + (m - 64) * d + p
                col = vtile_col(g, p * ntile_ph + i)
                vb[:, col] = np.where((tok >= 0) & (tok < S), 0.0, NEG)
    return vb


def na_bias_tables(rpb, j):
    out = np.full((5, 4, 128, 6, 256), NEG, np.float32)
    rpb = np.asarray(rpb, np.float32)
    reps = {0: 0, 1: 1, 2: 14, 3: 15, 4: 7}
    a = np.arange(128)
    for slot, qi in reps.items():
        gq = j * 16 + qi
        qtok = gq * 128 + np.arange(128)
        r = qtok // 64
        cc = qtok % 64
        rs = np.clip(r - 4, 0, 128 - 8)
        cs = np.clip(cc - 8, 0, 64 - 16)
        offs = na_offsets(qi)
        for jj, off in enumerate(offs):
            gk = gq + off
            if gk < 0 or gk >= 64:
                continue
            ktok = gk * 128 + a
            kr = ktok // 64
            kc = ktok % 64
            valid = ((kr[:, None] >= rs[None, :]) & (kr[:, None] < rs[None, :] + 8) &
                     (kc[:, None] >= cs[None, :]) & (kc[:, None] < cs[None, :] + 16))
            rr = np.clip(kr[:, None] - r[None, :] + 7, 0, 14)
            cr = np.clip(kc[:, None] - cc[None, :] + 15, 0, 30)
            for h in range(8):
                vals = rpb[h][rr, cr]
                tile = np.where(valid, vals, np.float32(NEG))
                out[slot, h // 2, :, jj, (h % 2) * 128:(h % 2 + 1) * 128] = tile
    return out


_PROGS = {}


def get_prog(kind):
    if kind not in _PROGS:
        b = Builder(kind)
        _PROGS[kind] = b.build()
    return _PROGS[kind]


def run(kind, in_maps):
    nc = get_prog(kind)
    res = run_bass_kernel_spmd(nc, in_maps, core_ids=list(range(NCORES)))
    return res.results


def make_exchange(xn_list):
    galls = []
    for b in range(2):
        galls.append(np.ascontiguousarray(np.concatenate([xn_list[b * 4 + j] for j in range(4)], axis=1)))
    gexts = []
    for c in range(NCORES):
        b, j = divmod(c, 4)
        ge = np.zeros((D, EXT), dtype=galls[b].dtype)
        lo = j * T - HALO
        hi = (j + 1) * T + HALO
        slo, shi = max(lo, 0), min(hi, S)
        ge[:, slo - lo:shi - lo] = galls[b][:, slo:shi]
        gexts.append(ge)
    return gexts, galls


def kernel(x, norm_mix, norm_mlp, norm_final, ev_w_in, ev_rpb, ev_q_norm, ev_w_uq, ev_kv_norm, ev_w_ukv,
           ev_w_o, od_w_in, od_w_o, mlp_w1, mlp_w2):
    x = np.asarray(x, np.float32)
    f = lambda a: np.ascontiguousarray(np.asarray(a, np.float32))
    xT = []
    for c in range(NCORES):
        b, j = divmod(c, 4)
        xT.append(np.ascontiguousarray(x[b, j * T:(j + 1) * T, :].T))
    ident = np.eye(128, dtype=np.float32)
    res = run("norm0", [{"xT": xT[c], "gnext": pc_layout(norm_mix[0], 8)} for c in range(NCORES)])
    xn = [res[c]["xn_bf"] for c in range(NCORES)]
    final = None
    for layer in range(4):
        gexts, galls = make_exchange(xn)
        gnext = norm_mix[layer + 1] if layer < 3 else norm_final
        mix_maps = []
        tail_maps = []
        for c in range(NCORES):
            b, j = divmod(c, 4)
            tm = {"xT": xT[c], "w1": f(mlp_w1[layer]), "w2": f(mlp_w2[layer]),
                  "g2": pc_layout(norm_mlp[layer], 8), "gnext": pc_layout(gnext, 8)}
            m = {"gext": gexts[c], "ident": ident}
            if layer % 2 == 0:
                e = layer // 2
                rkC, rkS = rope_table_np(0, S, 32, 32)
                rqC, rqS = rope_table_np(j * T, T, 32, 32)
                m.update({"gall": galls[b], "w_in": f(ev_w_in[e]), "w_uq": f(ev_w_uq[e]), "w_ukv": f(ev_w_ukv[e]),
                          "qn": pc_layout(ev_q_norm[e], 2), "kvn": pc_layout(ev_kv_norm[e], 1),
                          "nab": na_bias_tables(ev_rpb[e], j), "rkC": rkC, "rkS": rkS, "rqC": rqC, "rqS": rqS,
                          "rot32": rot_matrix(1, 32)})
                tm["w_o"] = f(ev_w_o[e])
            else:
                o = layer // 2
                rC, rS = rope_table_np(j * T - HALO, EXT, 64, 128)
                m.update({"w_in": f(od_w_in[o]), "ropeC": rC, "ropeS": rS,
                          "rot64": rot_matrix(2, 64), "vbias": dil_vbias(j), "masks": dil_masks()})
                tm["w_o"] = f(od_w_o[o])
            mix_maps.append(m)
            tail_maps.append(tm)
        res = run("even_mix" if layer % 2 == 0 else "odd_mix", mix_maps)
        for c in range(NCORES):
            tail_maps[c]["mixT"] = res[c]["mixT"]
        res = run("tail", tail_maps)
        xT = [res[c]["xoT"] for c in range(NCORES)]
        xn = [res[c]["xn_bf"] for c in range(NCORES)]
        final = [res[c]["xn_f32"] for c in range(NCORES)]
    out = np.empty((2, S, D), np.float32)
    for c in range(NCORES):
        b, j = divmod(c, 4)
        out[b, j * T:(j + 1) * T, :] = final[c].T
    return out
```

```python
import math
from contextlib import ExitStack

import numpy as np
import ml_dtypes
import concourse.bass as bass
import concourse.mybir as mybir
from concourse.bass_utils import run_bass_kernel_spmd

F32 = mybir.dt.float32
BF16 = mybir.dt.bfloat16
AF = mybir.ActivationFunctionType
ALU = mybir.AluOpType

ENGS = ("pe", "act", "dve", "pool", "sp")
NDMA_SEMS = 12

D = 1024
T = 2048
S = 8192
NCORES = 8
HALO = 1024
EXT = T + 2 * HALO
EPS = 1e-6
NEG = -30000.0
DIL = ((128, 1), (512, 4), (2048, 16))
ARENA_ELEMS = 103 * 1024


class Op:
    __slots__ = ("eng", "fn", "reads", "writes", "dma", "deps", "sem", "semval",
                 "has_dep", "prewait", "barrier")

    def __init__(self, eng, fn, reads, writes, dma):
        self.eng = eng
        self.fn = fn
        self.reads = reads
        self.writes = writes
        self.dma = dma
        self.deps = set()
        self.has_dep = False
        self.sem = None
        self.semval = 0
        self.prewait = None
        self.barrier = False


class Prog:
    def __init__(self, nc):
        self.nc = nc
        self.ops = []
        self.last_w = {}
        self.readers = {}
        self.last_op = {}
        self.dmas_since_barrier = []

    def op(self, eng, fn, reads=(), writes=(), dma=False):
        o = Op(eng, fn, tuple(reads), tuple(writes), dma)
        idx = len(self.ops)
        deps = set()
        for r in o.reads:
            w = self.last_w.get(r)
            if w is not None:
                deps.add(w)
        for wkey in o.writes:
            w = self.last_w.get(wkey)
            if w is not None:
                deps.add(w)
            for rd in self.readers.get(wkey, ()):
                deps.add(rd)
        deps.discard(idx)
        if eng == "pe" and not dma:
            deps = {d for d in deps if not (self.ops[d].eng == "pe" and not self.ops[d].dma)}
        o.deps = deps
        for d in deps:
            self.ops[d].has_dep = True
        for r in o.reads:
            self.readers.setdefault(r, []).append(idx)
        for wkey in o.writes:
            self.last_w[wkey] = idx
            self.readers[wkey] = []
        self.ops.append(o)
        if dma:
            self.dmas_since_barrier.append(idx)
        else:
            self.last_op[eng] = idx
        return idx

    def pe(self, fn, reads=(), writes=()):
        return self.op("pe", fn, reads, writes)

    def act(self, fn, reads=(), writes=()):
        return self.op("act", fn, reads, writes)

    def dve(self, fn, reads=(), writes=()):
        return self.op("dve", fn, reads, writes)

    def pool(self, fn, reads=(), writes=()):
        return self.op("pool", fn, reads, writes)

    def dma(self, eng, fn, reads=(), writes=()):
        return self.op(eng, fn, reads, writes, dma=True)

    def barrier(self):
        deps = set(self.last_op.values()) | set(self.dmas_since_barrier)
        for d in deps:
            self.ops[d].has_dep = True
        for e in ENGS:
            o = Op(e, None, (), (), False)
            o.barrier = True
            o.deps = set(deps)
            self.ops.append(o)
        self.last_w = {}
        self.readers = {}
        self.dmas_since_barrier = []

    def emit(self, final_wait_ops=()):
        nc = self.nc
        with ExitStack() as es:
            esem = {e: es.enter_context(nc.semaphore("s_" + e)) for e in ENGS}
            dsem = {e: [es.enter_context(nc.semaphore("d_%s%d" % (e, i))) for i in range(NDMA_SEMS)]
                    for e in ("sp", "pool", "act")}
            ecount = {e: 0 for e in ENGS}
            dcount = {e: 0 for e in dsem}
            dval = {e: [0] * NDMA_SEMS for e in dsem}
            for o in self.ops:
                if o.barrier:
                    continue
                if o.dma:
                    k = dcount[o.eng]
                    slot = k % NDMA_SEMS
                    dcount[o.eng] += 1
                    o.sem = dsem[o.eng][slot]
                    if dval[o.eng][slot] > 0:
                        o.prewait = (o.sem, dval[o.eng][slot])
                    dval[o.eng][slot] += 16
                    o.semval = dval[o.eng][slot]
                elif o.has_dep:
                    ecount[o.eng] += 1
                    o.sem = esem[o.eng]
                    o.semval = ecount[o.eng]
            streams = {e: [] for e in ENGS}
            for i, o in enumerate(self.ops):
                streams[o.eng].append(i)
            final_waits = [(self.ops[i].sem, self.ops[i].semval) for i in final_wait_ops]
            ops = self.ops
            self.stats = {e: len(streams[e]) for e in ENGS}
            self.stats["sem_max"] = dict(ecount)

            def run_stream(e, eng):
                waited = {}
                for i in streams[e]:
                    o = ops[i]
                    need = {}
                    if o.prewait is not None:
                        need[id(o.prewait[0])] = o.prewait
                    for d in o.deps:
                        do = ops[d]
                        key = id(do.sem)
                        cur = need.get(key)
                        if cur is None or cur[1] < do.semval:
                            need[key] = (do.sem, do.semval)
                    for key, (s, v) in need.items():
                        if waited.get(key, 0) >= v:
                            continue
                        eng.wait_ge(s, v)
                        waited[key] = v
                    if o.barrier:
                        continue
                    ins = o.fn(eng)
                    if o.sem is not None:
                        ins.then_inc(o.sem, 16 if o.dma else 1)
                if e == "sp":
                    for s, v in final_waits:
                        eng.wait_ge(s, v)

            with nc.Block() as block:
                @block.sync
                def _(eng):
                    run_stream("sp", eng)

                @block.tensor
                def _(eng):
                    run_stream("pe", eng)

                @block.scalar
                def _(eng):
                    run_stream("act", eng)

                @block.vector
                def _(eng):
                    run_stream("dve", eng)

                @block.gpsimd
                def _(eng):
                    run_stream("pool", eng)


class Arena:
    def __init__(self, tensor, nelem):
        self.t = tensor
        self.n = nelem
        self.top = 0

    def alloc(self, shape, dt):
        free = 1
        for s in shape[1:]:
            free *= s
        n16 = free * (2 if dt == F32 else 1)
        n16 = (n16 + 15) // 16 * 16
        off = self.top
        self.top += n16
        assert self.top <= self.n, ("SBUF arena overflow", self.top, self.n)
        v = self.t[:, off:off + free * (2 if dt == F32 else 1)]
        if dt == F32:
            v = v.bitcast(F32)
        if len(shape) == 3:
            v = v.rearrange("p (a b) -> p a b", a=shape[1])
        elif len(shape) == 4:
            v = v.rearrange("p (a b c) -> p a b c", a=shape[1], b=shape[2])
        return v

    def mark(self):
        return self.top

    def release(self, m):
        self.top = m


class Builder:
    def __init__(self, kind):
        self.kind = kind
        self.nc = bass.Bass("TRN2", target_bir_lowering=False)
        self.P = Prog(self.nc)
        self.dram = {}
        self.uid = 0
        self.rr = 0
        self.stop = None

    def din(self, name, shape, dt=F32):
        t = self.nc.dram_tensor(name, list(shape), dt, kind="ExternalInput").ap()
        self.dram[name] = t
        return t

    def dout(self, name, shape, dt=F32):
        t = self.nc.dram_tensor(name, list(shape), dt, kind="ExternalOutput").ap()
        self.dram[name] = t
        return t

    def dint(self, name, shape, dt=F32):
        t = self.nc.dram_tensor(name, list(shape), dt, kind="Internal").ap()
        self.dram[name] = t
        return t

    def key(self, base):
        self.uid += 1
        return "%s#%d" % (base, self.uid)

    def load_w(self, dst, src_ap, key):
        self.P.dma("pool", lambda e, d=dst, s=src_ap: e.dma_start(out=d, in_=s), writes=[key])

    def load(self, dst, src_ap, key, q="sp"):
        self.P.dma(q, lambda e, d=dst, s=src_ap: e.dma_start(out=d, in_=s), writes=[key])

    def evac(self, dst, src, reads, writes, which=None):
        if which is None:
            self.rr ^= 1
            which = "act" if self.rr else "dve"
        if which == "act":
            self.P.act(lambda e, d=dst, s=src: e.activation(out=d, in_=s, func=AF.Copy), reads, writes)
        else:
            self.P.dve(lambda e, d=dst, s=src: e.tensor_copy(out=d, in_=s), reads, writes)

    def rmsnorm_fm(self, A, xsrc, xkeys, gain, nfeat_chunks, ntok, dst, dkeys, ps_bank, ones,
                   dst32=None, blk=512, nfeat=None):
        P = self.P
        C = nfeat_chunks
        nfeat = nfeat or C * 128
        m = A.mark()
        sq = [A.alloc([128, C, blk], BF16) for _ in range(2)]
        sr = [A.alloc([128, blk], F32) for _ in range(2)]
        rr = [A.alloc([128, blk], F32) for _ in range(2)]
        ksq = [self.key("sq"), self.key("sq")]
        ksr = [self.key("sr"), self.key("sr")]
        krr = [self.key("rr"), self.key("rr")]
        pskey = "ps%d" % ps_bank
        for b in range(ntok // blk):
            i = b % 2
            sl = slice(b * blk, (b + 1) * blk)
            P.act(lambda e, o=sq[i], x=xsrc[:, :, sl]: e.activation(out=o, in_=x, func=AF.Square),
                  [xkeys[b]], [ksq[i]])
            for c in range(C):
                P.pe(lambda e, c=c, i=i: e.matmul(self.ps[:, ps_bank, 0:blk], ones[:, :], sq[i][:, c, :],
                                                 start=(c == 0), stop=(c == C - 1)),
                     [ksq[i], "ones"], [pskey])
            P.act(lambda e, o=sr[i]: e.activation(out=o, in_=self.ps[:, ps_bank, 0:blk], func=AF.Sqrt,
                                                 bias=self.eps_t[:, 0:1], scale=1.0 / nfeat),
                  [pskey], [ksr[i]])
            P.dve(lambda e, o=rr[i], s=sr[i]: e.reciprocal(out=o, in_=s), [ksr[i]], [krr[i]])
            for c in range(C):
                P.dve(lambda e, c=c, i=i, sl=sl: e.scalar_tensor_tensor(
                    out=dst[:, c, sl], in0=xsrc[:, c, sl], scalar=gain[:, c:c + 1], in1=rr[i],
                    op0=ALU.mult, op1=ALU.mult), [xkeys[b], krr[i], "gains"], [dkeys[b]])
                if dst32 is not None:
                    P.pool(lambda e, c=c, i=i, sl=sl: e.tensor_tensor(
                        out=dst32[:, c, sl], in0=xsrc[:, c, sl], in1=rr[i], op=ALU.mult),
                        [xkeys[b], krr[i]], [dkeys[b] + "f"])

    def build(self):
        nc = self.nc
        P = self.P
        kind = self.kind
        with ExitStack() as es:
            arena_t = es.enter_context(nc.sbuf_tensor("arena", [128, ARENA_ELEMS], BF16))
            self.ps = es.enter_context(nc.psum_tensor("ps", [128, 8, 512], F32))
            A = Arena(arena_t, ARENA_ELEMS)
            self.A = A
            self.eps_t = A.alloc([128, 1], F32)
            P.dve(lambda e: e.memset(self.eps_t, EPS), [], ["eps"])
            ones = A.alloc([128, 128], BF16)
            self.ones = ones
            P.dve(lambda e: e.memset(ones, 1.0), [], ["ones"])
            if kind == "norm0":
                outs = self.build_norm0(A)
            else:
                outs = self.build_layer(A)
            P.barrier()
            P.emit(final_wait_ops=outs)
        return nc

    def build_norm0(self, A):
        P = self.P
        xT = self.din("xT", [D, T])
        g = self.din("gnext", [128, 8])
        xn_o = self.dout("xn_bf", [D, T], BF16)
        gains = A.alloc([128, 8], F32)
        self.load(gains, g, "gains")
        xres = A.alloc([128, 8, T], F32)
        xkeys = ["xres%d" % b for b in range(4)]
        for b in range(4):
            self.load(xres[:, :, b * 512:(b + 1) * 512],
                      xT.rearrange("(c p) t -> p c t", p=128)[:, :, b * 512:(b + 1) * 512], xkeys[b])
        xn = A.alloc([128, 8, T], BF16)
        dkeys = ["xn%d" % b for b in range(4)]
        self.rmsnorm_fm(A, xres, xkeys, gains, 8, T, xn, dkeys, 0, self.ones)
        outs = []
        for b in range(4):
            outs.append(P.dma("sp", lambda e, b=b: e.dma_start(
                out=xn_o.rearrange("(c p) t -> p c t", p=128)[:, :, b * 512:(b + 1) * 512],
                in_=xn[:, :, b * 512:(b + 1) * 512]), [dkeys[b]], []))
        return outs

    def build_layer(self, A):
        P = self.P
        kind = self.kind
        self.mix_stores = []
        if kind == "tail":
            xT = self.din("xT", [D, T])
            self.mixT = self.din("mixT", [D, T], BF16)
            return self.tail(A, xT)
        gext = self.din("gext", [D, EXT], BF16)
        self.gext = gext
        if kind in ("even", "odd"):
            xT = self.din("xT", [D, T])
            self.mixT = self.dint("mixT", [D, T], BF16)
        else:
            self.mixT = self.dout("mixT", [D, T], BF16)
        self.identb = A.alloc([128, 128], BF16)
        ident_d = self.din("ident", [128, 128])
        self.load_w(self.identb, ident_d, "ident")
        m0 = A.mark()
        if kind.startswith("even"):
            self.even_mixer(A)
        else:
            self.odd_mixer(A)
        P.barrier()
        A.release(m0)
        if kind.endswith("_mix"):
            return self.mix_stores
        return self.tail(A, xT)

    def tail(self, A, xT):
        P = self.P
        ps = self.ps
        w_o = self.din("w_o", [D, D])
        w1 = self.din("w1", [D, 4 * D])
        w2 = self.din("w2", [4 * D, D])
        g2 = self.din("g2", [128, 8])
        gn = self.din("gnext", [128, 8])
        xo = self.dout("xoT", [D, T])
        xn_o = self.dout("xn_bf", [D, T], BF16)
        xn32_o = self.dout("xn_f32", [D, T])
        gains2 = A.alloc([128, 8], F32)
        gainsn = A.alloc([128, 8], F32)
        self.load(gains2, g2, "gains", q=getattr(self, "gq", "sp"))
        self.load(gainsn, gn, "gains", q=getattr(self, "gq", "sp"))
        xres = A.alloc([128, 8, T], F32)
        xkeys = ["xres%d" % b for b in range(4)]
        lvl = getattr(self, "tail_lvl", 9)
        for b in range(4 if lvl >= 1 else 0):
            self.load(xres[:, :, b * 512:(b + 1) * 512],
                      xT.rearrange("(c p) t -> p c t", p=128)[:, :, b * 512:(b + 1) * 512], xkeys[b])
        m1 = A.mark()
        mix = A.alloc([128, 8, T], BF16)
        for c in range(8 if lvl >= 2 else 0):
            self.load(mix[:, c, :], self.mixT[c * 128:(c + 1) * 128, :], "mix%d" % c)
        wo = A.alloc([128, 8, D], BF16)
        for h in range(2 if lvl >= 3 else 0):
            self.load_w(wo[:, :, h * 512:(h + 1) * 512],
                        w_o.rearrange("(kc p) n -> p kc n", p=128)[:, :, h * 512:(h + 1) * 512], "wo%d" % h)
        bank = 0
        for co in range(8 if not getattr(self, "no_wo_mm", False) else 0):
            for b in range(4):
                pk = "ps%d" % bank
                for kc in range(8):
                    P.pe(lambda e, co=co, b=b, kc=kc, bank=bank: e.matmul(
                        ps[:, bank, :], wo[:, kc, co * 128:(co + 1) * 128], mix[:, kc, b * 512:(b + 1) * 512],
                        start=(kc == 0), stop=(kc == 7)),
                        ["wo%d" % (co // 4), "mix%d" % kc], [pk])
                P.dve(lambda e, co=co, b=b, bank=bank: e.tensor_tensor(
                    out=xres[:, co, b * 512:(b + 1) * 512], in0=ps[:, bank, :],
                    in1=xres[:, co, b * 512:(b + 1) * 512], op=ALU.add), [pk, xkeys[b]], [xkeys[b]])
                bank = (bank + 1) % 4
        P.barrier()
        A.release(m1)
        if self.stop == "wo":
            return []
        xn2 = A.alloc([128, 8, T], BF16)
        nkeys = ["xn2_%d" % b for b in range(4)]
        self.rmsnorm_fm(A, xres, xkeys, gains2, 8, T, xn2, nkeys, 7, self.ones)
        hT = A.alloc([128, 4, T], BF16)
        w1b = [A.alloc([128, 8, 512], BF16) for _ in range(2)]
        w2b = [A.alloc([128, 4, D], BF16) for _ in range(2)]
        rl = [A.alloc([128, 512], F32) for _ in range(3)]
        NFB = 8
        bank = 0
        rli = 0
        for fb in range(NFB):
            i = fb % 2
            self.load_w(w1b[i], w1.rearrange("(kc p) n -> p kc n", p=128)[:, :, fb * 512:(fb + 1) * 512],
                        "w1b%d" % i)
            self.load_w(w2b[i], w2[fb * 512:(fb + 1) * 512, :].rearrange("(fc p) n -> p fc n", p=128),
                        "w2b%d" % i)
            for b in range(4):
                for fc in range(4):
                    pk = "ps%d" % bank
                    for kc in range(8):
                        P.pe(lambda e, i=i, fc=fc, b=b, kc=kc, bank=bank: e.matmul(
                            ps[:, bank, :], w1b[i][:, kc, fc * 128:(fc + 1) * 128],
                            xn2[:, kc, b * 512:(b + 1) * 512], start=(kc == 0), stop=(kc == 7)),
                            ["w1b%d" % i, nkeys[b]], [pk])
                    rk = "rl%d" % rli
                    P.act(lambda e, bank=bank, r=rl[rli]: e.activation(out=r, in_=ps[:, bank, :], func=AF.Relu),
                          [pk], [rk])
                    P.pool(lambda e, fc=fc, b=b, r=rl[rli]: e.tensor_tensor(
                        out=hT[:, fc, b * 512:(b + 1) * 512], in0=r, in1=r, op=ALU.mult),
                        [rk], ["hT%d_%d" % (fc, b)])
                    rli = (rli + 1) % 3
                    bank = (bank + 1) % 3
            for co in range(8):
                for b in range(4):
                    pk = "ps%d" % (4 + bank % 3)
                    bb = 4 + bank % 3
                    for fc in range(4):
                        P.pe(lambda e, i=i, fc=fc, b=b, co=co, bb=bb: e.matmul(
                            ps[:, bb, :], w2b[i][:, fc, co * 128:(co + 1) * 128],
                            hT[:, fc, b * 512:(b + 1) * 512], start=(fc == 0), stop=(fc == 3)),
                            ["w2b%d" % i, "hT%d_%d" % (fc, b)], [pk])
                    P.dve(lambda e, co=co, b=b, bb=bb: e.tensor_tensor(
                        out=xres[:, co, b * 512:(b + 1) * 512], in0=ps[:, bb, :],
                        in1=xres[:, co, b * 512:(b + 1) * 512], op=ALU.add), [pk, xkeys[b]], [xkeys[b]])
                    bank = (bank + 1) % 3
        P.barrier()
        A.release(m1)
        if self.stop == "mlp":
            return []
        xn = A.alloc([128, 8, T], BF16)
        xn32 = A.alloc([128, 8, 512], F32)
        outs = []
        for b in range(4):
            outs.append(P.dma("sp", lambda e, b=b: e.dma_start(
                out=xo.rearrange("(c p) t -> p c t", p=128)[:, :, b * 512:(b + 1) * 512],
                in_=xres[:, :, b * 512:(b + 1) * 512]), [xkeys[b]], []))
        dkeys = ["xnn%d" % b for b in range(4)]
        self.rmsnorm_fm(A, xres, xkeys, gainsn, 8, T, xn, dkeys, 7, self.ones)
        for b in range(4):
            outs.append(P.dma("sp", lambda e, b=b: e.dma_start(
                out=xn_o.rearrange("(c p) t -> p c t", p=128)[:, :, b * 512:(b + 1) * 512],
                in_=xn[:, :, b * 512:(b + 1) * 512]), [dkeys[b]], []))
        m2 = A.mark()
        sq = A.alloc([128, 8, 512], BF16)
        sr = A.alloc([128, 512], F32)
        rr = A.alloc([128, 512], F32)
        for b in range(4):
            sl = slice(b * 512, (b + 1) * 512)
            P.act(lambda e, sl=sl: e.activation(out=sq, in_=xres[:, :, sl], func=AF.Square), [xkeys[b]], ["fsq"])
            for c in range(8):
                P.pe(lambda e, c=c: e.matmul(ps[:, 6, :], self.ones[:, :], sq[:, c, :], start=(c == 0), stop=(c == 7)),
                     ["fsq", "ones"], ["ps6"])
            P.act(lambda e: e.activation(out=sr, in_=ps[:, 6, :], func=AF.Sqrt, bias=self.eps_t[:, 0:1],
                                         scale=1.0 / D), ["ps6"], ["fsr"])
            P.dve(lambda e: e.reciprocal(out=rr, in_=sr), ["fsr"], ["frr"])
            for c in range(8):
                P.dve(lambda e, c=c, sl=sl: e.scalar_tensor_tensor(
                    out=xn32[:, c, :], in0=xres[:, c, sl], scalar=gainsn[:, c:c + 1], in1=rr,
                    op0=ALU.mult, op1=ALU.mult), [xkeys[b], "frr", "gains"], ["xn32"])
            outs.append(P.dma("sp", lambda e, sl=sl: e.dma_start(
                out=xn32_o.rearrange("(c p) t -> p c t", p=128)[:, :, sl], in_=xn32), ["xn32"], []))
        A.release(m2)
        return outs

    def rope_block(self, A, bufs, psrc, pskey, n, rot_lhsT, rot_key, Ct, St, tkeys, dst, dkey, prow, pr_bank,
                   extra=()):
        P = self.P
        ps = self.ps
        qb, t1, t2, kq, k1, k2 = bufs
        P.act(lambda e: e.activation(out=qb[prow, 0:n], in_=psrc, func=AF.Copy), [pskey] + list(extra), [kq])
        prk = "ps%d" % pr_bank
        P.pe(lambda e: e.matmul(ps[prow, pr_bank, 0:n], rot_lhsT, qb[prow, 0:n], start=True, stop=True),
             [kq, rot_key], [prk])
        P.dve(lambda e: e.tensor_tensor(out=t1[prow, 0:n], in0=psrc, in1=Ct, op=ALU.mult),
              [pskey, kq] + tkeys, [k1])
        P.dve(lambda e: e.tensor_tensor(out=t2[prow, 0:n], in0=ps[prow, pr_bank, 0:n], in1=St, op=ALU.mult),
              [prk] + tkeys, [k2])
        P.pool(lambda e: e.tensor_tensor(out=dst, in0=t1[prow, 0:n], in1=t2[prow, 0:n], op=ALU.add),
               [k1, k2], [dkey])

    def rope_bufs(self, A, nsets=2):
        sets = []
        for _ in range(nsets):
            sets.append((A.alloc([128, 512], BF16), A.alloc([128, 512], F32), A.alloc([128, 512], F32),
                         self.key("rqb"), self.key("rt1"), self.key("rt2")))
        return sets

    def odd_mixer(self, A):
        P = self.P
        ps = self.ps
        gext = self.gext
        w_in = self.din("w_in", [D, 9216])
        ropeC = self.din("ropeC", [128, EXT])
        ropeS = self.din("ropeS", [128, EXT])
        rot_d = self.din("rot64", [128, 128])
        vb_d = self.din("vbias", [128, 72])
        masks_d = self.din("masks", [128, 512])
        gv = gext.rearrange("(c p) t -> p c t", p=128)
        xn = A.alloc([128, 8, T], BF16)
        for b in range(4):
            self.load(xn[:, :, b * 512:(b + 1) * 512], gv[:, :, HALO + b * 512:HALO + (b + 1) * 512], "xn%d" % b)
        Ct = A.alloc([128, EXT], F32)
        St = A.alloc([128, EXT], F32)
        self.load(Ct, ropeC, "ropeT", q="sp")
        self.load(St, ropeS, "ropeT", q="sp")
        rot = A.alloc([128, 128], BF16)
        self.load_w(rot, rot_d, "rot")
        vb = A.alloc([128, 72], F32)
        self.load(vb, vb_d, "vb")
        masks = A.alloc([128, 512], BF16)
        self.load_w(masks, masks_d, "masks")
        xh = [A.alloc([128, 8, 512], BF16) for _ in range(2)]
        wq = [A.alloc([128, 8, 128], BF16) for _ in range(2)]
        wk = [A.alloc([128, 8, 128], BF16) for _ in range(2)]
        wv = [A.alloc([128, 8, 128], BF16) for _ in range(2)]
        QTm = [A.alloc([128, T], BF16) for _ in range(2)]
        P.pool(lambda e: e.memset(QTm[0][64:128, :], 0.0), [], ["QT"])
        P.pool(lambda e: e.memset(QTm[1][0:64, :], 0.0), [], ["QT"])
        KT = A.alloc([128, EXT + 128], BF16)
        VT = A.alloc([128, EXT + 128], BF16)
        Vt = A.alloc([128, 33, 128], BF16)
        acc = A.alloc([128, T], F32)
        zacc = A.alloc([128, T], F32)
        rz = A.alloc([128, T], F32)
        mixc = [A.alloc([128, T], BF16) for _ in range(2)]
        PT = [A.alloc([128, 256], BF16) for _ in range(3)]
        rb = self.rope_bufs(A, 2)
        wv_w = w_in.rearrange("(kc p) n -> p kc n", p=128)
        if self.stop == "o0":
            return
        it = 0
        xhi = 0
        rbi = 0
        pti = 0
        proj_bank = 0
        s_bank = 0
        o_bank = 0
        vtile_base = [0, 0, 0]
        for c in range(getattr(self, "nchunks", 8)):
            for g in getattr(self, "gorder", (0, 1, 2)):
                w, d = DIL[g]
                halo = 64 * d
                L = T // d + 128
                npos = T // d
                wi = it % 2
                it += 1
                col = g * 3072 + c * 128
                self.load_w(wq[wi], wv_w[:, :, col:col + 128], "wq%d" % wi)
                self.load_w(wk[wi], wv_w[:, :, col + 1024:col + 1024 + 128], "wk%d" % wi)
                self.load_w(wv[wi], wv_w[:, :, col + 2048:col + 2048 + 128], "wv%d" % wi)
                KTv = KT[:, 0:d * L].rearrange("p (dd m) -> p dd m", dd=d)
                VTv = VT[:, 0:d * L].rearrange("p (dd m) -> p dd m", dd=d)
                QTv = [QTm[hh][:, :].rearrange("p (dd m) -> p dd m", dd=d) for hh in range(2)]
                blocks = []
                hb = min(halo, 512)
                for u0 in range(0, halo, hb):
                    blocks.append(("halo", u0, hb, HALO - halo + u0))
                for b in range(4):
                    blocks.append(("own", halo + b * 512, 512, b))
                for u0 in range(0, halo, hb):
                    blocks.append(("halo", halo + T + u0, hb, HALO + T + u0))
                for (src, u0, n, info) in blocks:
                    if src == "halo":
                        xi = xhi % 2
                        xhi += 1
                        xk = "xh%d" % xi
                        self.load(xh[xi][:, :, 0:n], gv[:, :, info:info + n], xk)
                        xs = xh[xi][:, :, 0:n]
                        xkeys = [xk]
                        tpos = info
                    else:
                        xs = xn[:, :, info * 512:(info + 1) * 512]
                        xkeys = ["xn%d" % info]
                        tpos = HALO + info * 512
                    m0, m1 = u0 // d, (u0 + n) // d
                    todo = [("k", wk[wi], "wk%d" % wi), ("v", wv[wi], "wv%d" % wi)]
                    if src == "own":
                        todo.append(("q", wq[wi], "wq%d" % wi))
                    for (what, wt, wkey) in todo:
                        pk = "ps%d" % proj_bank
                        pb = proj_bank
                        proj_bank ^= 1
                        for kc in range(8):
                            P.pe(lambda e, wt=wt, kc=kc, xs=xs, n=n, pb=pb: e.matmul(
                                ps[:, pb, 0:n], wt[:, kc, :], xs[:, kc, :], start=(kc == 0), stop=(kc == 7)),
                                [wkey] + xkeys, [pk])
                        src_v = ps[:, pb, 0:n].rearrange("p (m dd) -> p dd m", dd=d)
                        dbg = getattr(self, "dbg", "")
                        if what == "v" or ("norope" in dbg and what == "k"):
                            if "novevac" not in dbg:
                                self.evac(VTv[:, :, m0:m1] if what == "v" else KTv[:, :, m0:m1], src_v, [pk],
                                          ["VT" if what == "v" else "KT"])
                        elif "norope" in dbg:
                            pass
                        else:
                            if what == "k":
                                dst = KTv[:, :, m0:m1]
                                dkey = "KT"
                            else:
                                q0 = (u0 - halo) // d
                                dst = None
                                dkey = "QT"
                            bufs = rb[rbi % 2]
                            rbi += 1
                            qb, t1, t2, kq, k1, k2 = bufs
                            P.act(lambda e, qb=qb, n=n, pb=pb: e.activation(out=qb[:, 0:n], in_=ps[:, pb, 0:n],
                                                                           func=AF.Copy), [pk], [kq])
                            P.pe(lambda e, qb=qb, n=n: e.matmul(ps[:, 2, 0:n], rot[:, :], qb[:, 0:n],
                                                               start=True, stop=True), [kq, "rot"], ["ps2"])
                            P.dve(lambda e, t1=t1, n=n, pb=pb, tpos=tpos: e.tensor_tensor(
                                out=t1[:, 0:n], in0=ps[:, pb, 0:n], in1=Ct[:, tpos:tpos + n], op=ALU.mult),
                                [pk, "ropeT", kq], [k1])
                            P.dve(lambda e, t2=t2, n=n, tpos=tpos: e.tensor_tensor(
                                out=t2[:, 0:n], in0=ps[:, 2, 0:n], in1=St[:, tpos:tpos + n], op=ALU.mult),
                                ["ps2", "ropeT"], [k2])
                            P.dve(lambda e, t1=t1, t2=t2, n=n: e.tensor_tensor(
                                out=t1[:, 0:n], in0=t1[:, 0:n], in1=t2[:, 0:n], op=ALU.add), [k1, k2], [k1])
                            if dst is not None:
                                P.act(lambda e, dst=dst, t1=t1, n=n, d=d: e.activation(
                                    out=dst, in_=t1[:, 0:n].rearrange("p (m dd) -> p dd m", dd=d), func=AF.Copy),
                                    [k1], [dkey])
                            else:
                                for hh in range(2):
                                    hs = slice(hh * 64, (hh + 1) * 64)
                                    P.act(lambda e, hs=hs, hh=hh, QTv=QTv, q0=q0, t1=t1, n=n, d=d: e.activation(
                                        out=QTv[hh][hs, :, q0:q0 + n // d],
                                        in_=t1[hs, 0:n].rearrange("p (m dd) -> p dd m", dd=d), func=AF.Copy),
                                        [k1], [dkey])
                if self.stop == "o1":
                    return
                ntile_ph = L // 128
                for p in range(d):
                    for i in range(ntile_ph):
                        ti = p * ntile_ph + i
                        pst = ps[:, 7, :].bitcast(BF16)
                        P.pe(lambda e, p=p, i=i, pst=pst, VTv=VTv: e.transpose(pst[:, 0:128], VTv[:, p, i * 128:(i + 1) * 128],
                                                                      self.identb[:, :]), ["VT", "ident"], ["ps7"])
                        self.evac(Vt[:, ti, :], pst[:, 0:128], ["ps7"], ["Vt"], which="dve")
                if self.stop == "o2":
                    return
                for p in range(d):
                    for qi in range(npos // 128):
                        ob = 5 + (o_bank % 2)
                        o_bank += 1
                        ok = "ps%d" % ob
                        for blk in range(2):
                            kti = qi + blk
                            ti = p * ntile_ph + kti
                            sb = 3 + (s_bank % 2)
                            s_bank += 1
                            sk = "ps%d" % sb
                            for hh in range(2):
                                hs = slice(hh * 64, (hh + 1) * 64)
                                P.pe(lambda e, hs=hs, hh=hh, p=p, kti=kti, qi=qi, sb=sb, KTv=KTv, QTv=QTv: e.matmul(
                                    ps[:, sb, hh * 128:(hh + 1) * 128], KTv[:, p, kti * 128:(kti + 1) * 128],
                                    QTv[hh][:, p, qi * 128:(qi + 1) * 128], start=True, stop=True),
                                    ["KT", "QT"], [sk])
                            pt = PT[pti % 3]
                            ptk = "PT%d" % (pti % 3)
                            pti += 1
                            vcol = vtile_col(g, ti)
                            P.act(lambda e, pt=pt, sb=sb, vcol=vcol: e.activation(
                                out=pt, in_=ps[:, sb, 0:256], func=AF.Exp, bias=vb[:, vcol:vcol + 1], scale=0.125),
                                [sk, "vb"], [ptk])
                            P.dve(lambda e, pt=pt, blk=blk: e.tensor_tensor(
                                out=pt, in0=pt, in1=masks[:, blk * 256:(blk + 1) * 256], op=ALU.mult),
                                [ptk, "masks"], [ptk])
                            P.pe(lambda e, pt=pt, ob=ob, blk=blk: e.matmul(
                                ps[:, ob, 128:384], self.ones[:, :], pt, start=(blk == 0), stop=(blk == 1),
                                skip_group_check=True), [ptk, "ones"], [ok])
                            for hh in range(2):
                                hs = slice(hh * 64, (hh + 1) * 64)
                                P.pe(lambda e, pt=pt, ob=ob, hh=hh, hs=hs, ti=ti, blk=blk: e.matmul(
                                    ps[hs, ob, 0:128], Vt[:, ti, hs], pt[:, hh * 128:(hh + 1) * 128],
                                    start=False, stop=(blk == 1), skip_group_check=True), [ptk, "Vt"], [ok])
                        accv = acc[:, :].rearrange("p (m dd) -> p dd m", dd=d)[:, p, qi * 128:(qi + 1) * 128]
                        zv = zacc[:, :].rearrange("p (m dd) -> p dd m", dd=d)[:, p, qi * 128:(qi + 1) * 128]
                        if g == 0:
                            P.dve(lambda e, accv=accv, ob=ob: e.tensor_copy(out=accv, in_=ps[:, ob, 0:128]),
                                  [ok], ["acc"])
                            P.dve(lambda e, zv=zv, ob=ob: e.tensor_copy(out=zv[0:64], in_=ps[0:64, ob, 128:256]),
                                  [ok], ["zacc"])
                            P.dve(lambda e, zv=zv, ob=ob: e.tensor_copy(out=zv[64:128], in_=ps[64:128, ob, 256:384]),
                                  [ok], ["zacc"])
                        else:
                            P.dve(lambda e, accv=accv, ob=ob: e.tensor_tensor(
                                out=accv, in0=ps[:, ob, 0:128], in1=accv, op=ALU.add), [ok, "acc"], ["acc"])
                            P.dve(lambda e, zv=zv, ob=ob: e.tensor_tensor(
                                out=zv[0:64], in0=ps[0:64, ob, 128:256], in1=zv[0:64], op=ALU.add),
                                [ok, "zacc"], ["zacc"])
                            P.dve(lambda e, zv=zv, ob=ob: e.tensor_tensor(
                                out=zv[64:128], in0=ps[64:128, ob, 256:384], in1=zv[64:128], op=ALU.add),
                                [ok, "zacc"], ["zacc"])
            mi = c % 2
            for b in range(4):
                sl = slice(b * 512, (b + 1) * 512)
                P.dve(lambda e, sl=sl: e.reciprocal(out=rz[:, sl], in_=zacc[:, sl]), ["zacc"], ["rz%d" % b])
                P.pool(lambda e, sl=sl, mi=mi: e.tensor_tensor(out=mixc[mi][:, sl], in0=acc[:, sl], in1=rz[:, sl],
                                                              op=ALU.mult), ["acc", "rz%d" % b], ["mixc%d" % mi])
            self.mix_stores.append(P.dma("sp", lambda e, c=c, mi=mi: e.dma_start(
                out=self.mixT[c * 128:(c + 1) * 128, :], in_=mixc[mi]), ["mixc%d" % mi], ["mixT"]))

    def even_mixer(self, A):
        P = self.P
        ps = self.ps
        gext = self.gext
        gall = self.din("gall", [D, S], BF16)
        w_in = self.din("w_in", [D, 1952])
        w_uq = self.din("w_uq", [256, 768])
        w_ukv = self.din("w_ukv", [128, 1024])
        qn_d = self.din("qn", [128, 2])
        kvn_d = self.din("kvn", [128, 1])
        nab_d = self.din("nab", [5, 4, 128, 6, 256])
        rkC = self.din("rkC", [32, S])
        rkS = self.din("rkS", [32, S])
        rqC = self.din("rqC", [32, T])
        rqS = self.din("rqS", [32, T])
        rot_d = self.din("rot32", [32, 32])
        gv = gext.rearrange("(c p) t -> p c t", p=128)
        gav = gall.rearrange("(c p) t -> p c t", p=128)
        wv_w = w_in.rearrange("(kc p) n -> p kc n", p=128)
        R = slice(64, 96)
        cqn = A.alloc([128, 2, T], BF16)
        gq = A.alloc([128, 2], F32)
        gkv = A.alloc([128, 1], F32)
        self.load(gq, qn_d, "gains")
        self.load(gkv, kvn_d, "gains")
        rot = A.alloc([128, 32], BF16)
        self.load_w(rot[R, :], rot_d, "rot")
        m_na = A.mark()
        xn = A.alloc([128, 8, T], BF16)
        for b in range(4):
            self.load(xn[:, :, b * 512:(b + 1) * 512], gv[:, :, HALO + b * 512:HALO + (b + 1) * 512], "xn%d" % b)
        xh = A.alloc([128, 8, 512], BF16)
        self.load(xh[:, :, 0:256], gv[:, :, HALO - 256:HALO], "xh")
        self.load(xh[:, :, 256:512], gv[:, :, HALO + T:HALO + T + 256], "xh")
        wql = A.alloc([128, 8, 256], BF16)
        self.load_w(wql, wv_w[:, :, 1536:1792], "wql")
        cq = A.alloc([128, 2, T], F32)
        pbk = 0
        for b in range(4):
            for cc in range(2):
                pk = "ps%d" % pbk
                for kc in range(8):
                    P.pe(lambda e, cc=cc, kc=kc, b=b, pbk=pbk: e.matmul(
                        ps[:, pbk, :], wql[:, kc, cc * 128:(cc + 1) * 128], xn[:, kc, b * 512:(b + 1) * 512],
                        start=(kc == 0), stop=(kc == 7)), ["wql", "xn%d" % b], [pk])
                self.evac(cq[:, cc, b * 512:(b + 1) * 512], ps[:, pbk, :], [pk], ["cq%d" % b])
                pbk ^= 1
        self.rmsnorm_fm(A, cq, ["cq%d" % b for b in range(4)], gq, 2, T, cqn, ["cqn%d" % b for b in range(4)],
                        7, self.ones)
        if self.stop == "na0":
            return
        wq = [A.alloc([128, 8, 128], BF16) for _ in range(2)]
        wk = [A.alloc([128, 8, 128], BF16) for _ in range(2)]
        wvv = [A.alloc([128, 8, 128], BF16) for _ in range(2)]
        QTm = [A.alloc([128, T], BF16) for _ in range(2)]
        P.pool(lambda e: e.memset(QTm[0][64:128, :], 0.0), [], ["QT"])
        P.pool(lambda e: e.memset(QTm[1][0:64, :], 0.0), [], ["QT"])
        NK = T + 512
        KT = A.alloc([128, NK], BF16)
        VT = A.alloc([128, NK], BF16)
        Vt = A.alloc([128, NK // 128, 128], BF16)
        nab = [A.alloc([128, 6, 256], F32) for _ in range(2)]
        sbuf_s = [A.alloc([128, 256], F32) for _ in range(2)]
        PT = [A.alloc([128, 256], BF16) for _ in range(3)]
        rzn = [A.alloc([128, 128], F32) for _ in range(2)]
        mixc = [A.alloc([128, T], BF16) for _ in range(2)]
        proj_bank = 0
        s_bank = 0
        o_bank = 0
        pti = 0
        nbi = 0
        for c in range(4):
            wi = c % 2
            col = c * 128
            self.load_w(wq[wi], wv_w[:, :, col:col + 128], "wq%d" % wi)
            self.load_w(wk[wi], wv_w[:, :, 512 + col:512 + col + 128], "wk%d" % wi)
            self.load_w(wvv[wi], wv_w[:, :, 1024 + col:1024 + col + 128], "wv%d" % wi)
            blocks = [("halo", 0, 256, xh[:, :, 0:256], ["xh"])]
            for b in range(4):
                blocks.append(("own", 256 + b * 512, 512, xn[:, :, b * 512:(b + 1) * 512], ["xn%d" % b]))
            blocks.append(("halo", 256 + T, 256, xh[:, :, 256:512], ["xh"]))
            for (src, u0, n, xs, xkeys) in blocks:
                todo = [(wk[wi], "wk%d" % wi, KT[:, u0:u0 + n], "KT"), (wvv[wi], "wv%d" % wi, VT[:, u0:u0 + n], "VT")]
                if src == "own":
                    todo.append((wq[wi], "wq%d" % wi, None, "QT"))
                for (wt, wkey, dst, dkey) in todo:
                    pb = proj_bank
                    proj_bank ^= 1
                    pk = "ps%d" % pb
                    for kc in range(8):
                        P.pe(lambda e, wt=wt, kc=kc, xs=xs, n=n, pb=pb: e.matmul(
                            ps[:, pb, 0:n], wt[:, kc, :], xs[:, kc, :], start=(kc == 0), stop=(kc == 7)),
                            [wkey] + xkeys, [pk])
                    if dst is not None:
                        self.evac(dst, ps[:, pb, 0:n], [pk], [dkey])
                    else:
                        self.evac(QTm[0][0:64, u0 - 256:u0 - 256 + n], ps[0:64, pb, 0:n], [pk], [dkey], which="dve")
                        self.evac(QTm[1][64:128, u0 - 256:u0 - 256 + n], ps[64:128, pb, 0:n], [pk], [dkey], which="dve")
            if self.stop == "na1":
                return
            for ti in range(NK // 128):
                pst = ps[:, 7, :].bitcast(BF16)
                P.pe(lambda e, ti=ti, pst=pst: e.transpose(pst[:, 0:128], VT[:, ti * 128:(ti + 1) * 128],
                                                          self.identb[:, :]), ["VT", "ident"], ["ps7"])
                self.evac(Vt[:, ti, :], pst[:, 0:128], ["ps7"], ["Vt"], which="dve")
            if self.stop == "na2":
                return
            for qi in range(16):
                offs = na_offsets(qi)
                slot = na_slot(qi)
                nb = nab[nbi % 2]
                nbk = "nab%d" % (nbi % 2)
                nbi += 1
                self.load(nb, nab_d[slot, c], nbk, q="sp")
                ob = 5 + (o_bank % 2)
                o_bank += 1
                ok = "ps%d" % ob
                for j, off in enumerate(offs):
                    kt = qi + off + 2
                    sb = 3 + (s_bank % 2)
                    ssb = sbuf_s[s_bank % 2]
                    ssk = "ssb%d" % (s_bank % 2)
                    s_bank += 1
                    sk = "ps%d" % sb
                    for hh in range(2):
                        hs = slice(hh * 64, (hh + 1) * 64)
                        P.pe(lambda e, hs=hs, hh=hh, kt=kt, qi=qi, sb=sb: e.matmul(
                            ps[:, sb, hh * 128:(hh + 1) * 128], KT[:, kt * 128:(kt + 1) * 128],
                            QTm[hh][:, qi * 128:(qi + 1) * 128], start=True, stop=True), ["KT", "QT"], [sk])
                    P.dve(lambda e, ssb=ssb, sb=sb, nb=nb, j=j: e.scalar_tensor_tensor(
                        out=ssb, in0=ps[:, sb, 0:256], scalar=0.125, in1=nb[:, j, :], op0=ALU.mult, op1=ALU.add),
                        [sk, nbk], [ssk])
                    pt = PT[pti % 3]
                    ptk = "PT%d" % (pti % 3)
                    pti += 1
                    P.act(lambda e, pt=pt, ssb=ssb: e.activation(out=pt, in_=ssb, func=AF.Exp), [ssk], [ptk])
                    last = (j == len(offs) - 1)
                    dbg = getattr(self, "dbg", "")
                    if "noones" not in dbg:
                        P.pe(lambda e, pt=pt, ob=ob, j=j, last=last: e.matmul(
                            ps[:, ob, 128:384], self.ones[:, :], pt, start=(j == 0), stop=last,
                            skip_group_check=True), [ptk, "ones"], [ok])
                    if "nopv" not in dbg:
                        for hh in range(2):
                            hs = slice(hh * 64, (hh + 1) * 64)
                            P.pe(lambda e, pt=pt, ob=ob, hh=hh, hs=hs, kt=kt, last=last, j=j, dbg=dbg: e.matmul(
                                ps[hs, ob, 0:128], Vt[:, kt, hs], pt[:, hh * 128:(hh + 1) * 128],
                                start=(j == 0 and "noones" in dbg), stop=last, skip_group_check=True), [ptk, "Vt"], [ok])
                rzz = rzn[qi % 2]
                rzk = "rzn%d" % (qi % 2)
                if "norecip" in dbg:
                    P.dve(lambda e, rzz=rzz, ob=ob, qi=qi, wi=wi: e.tensor_copy(
                        out=mixc[wi][:, qi * 128:(qi + 1) * 128], in_=ps[:, ob, 0:128]), [ok], ["mixc%d" % wi])
                    continue
                P.dve(lambda e, rzz=rzz, ob=ob: e.reciprocal(out=rzz[0:64, :], in_=ps[0:64, ob, 128:256]), [ok], [rzk])
                P.dve(lambda e, rzz=rzz, ob=ob: e.reciprocal(out=rzz[64:128, :], in_=ps[64:128, ob, 256:384]),
                      [ok], [rzk])
                P.dve(lambda e, rzz=rzz, ob=ob, qi=qi, wi=wi: e.tensor_tensor(
                    out=mixc[wi][:, qi * 128:(qi + 1) * 128], in0=ps[:, ob, 0:128], in1=rzz, op=ALU.mult),
                    [ok, rzk], ["mixc%d" % wi])
            self.mix_stores.append(P.dma("sp", lambda e, c=c, wi=wi: e.dma_start(
                out=self.mixT[c * 128:(c + 1) * 128, :], in_=mixc[wi]), ["mixc%d" % wi], ["mixT"]))
        P.barrier()
        A.release(m_na)
        if self.stop == "na":
            return
        wl = A.alloc([128, 8, 160], BF16)
        self.load_w(wl, wv_w[:, :, 1792:1952], "wl")
        wuq = A.alloc([128, 2, 768], BF16)
        self.load_w(wuq, w_uq.rearrange("(kc p) n -> p kc n", p=128), "wuq")
        wukv = A.alloc([128, 1024], BF16)
        self.load_w(wukv, w_ukv, "wukv")
        ckvn = A.alloc([128, S], BF16)
        KTm = A.alloc([128, S], BF16)
        P.pool(lambda e: e.memset(KTm[64:128, :], 0.0), [], ["KTpe%d" % kb for kb in range(16)])
        Cq = A.alloc([128, T], F32)
        Sq = A.alloc([128, T], F32)
        self.load(Cq[R, :], rqC, "ropeq", q="sp")
        self.load(Sq[R, :], rqS, "ropeq", q="sp")
        m_lat = A.mark()
        gb = [A.alloc([128, 8, 512], BF16) for _ in range(2)]
        Ck = [A.alloc([128, 512], F32) for _ in range(2)]
        Sk = [A.alloc([128, 512], F32) for _ in range(2)]
        sq = [A.alloc([128, 512], BF16) for _ in range(2)]
        sr = [A.alloc([128, 512], F32) for _ in range(2)]
        rr = [A.alloc([128, 512], F32) for _ in range(2)]
        rb = self.rope_bufs(A, 2)
        for kb in range(16):
            i = kb % 2
            sl = slice(kb * 512, (kb + 1) * 512)
            self.load(gb[i], gav[:, :, sl], "gb%d" % i)
            self.load(Ck[i][R, :], rkC[:, sl], "rk%d" % i, q="sp")
            self.load(Sk[i][R, :], rkS[:, sl], "rk%d" % i, q="sp")
            pc = i
            pck = "ps%d" % pc
            for kc in range(8):
                P.pe(lambda e, kc=kc, i=i, pc=pc: e.matmul(ps[:, pc, :], wl[:, kc, 0:128], gb[i][:, kc, :],
                                                          start=(kc == 0), stop=(kc == 7)), ["wl", "gb%d" % i], [pck])
            for kc in range(8):
                P.pe(lambda e, kc=kc, i=i: e.matmul(ps[R, 2, :], wl[:, kc, 128:160], gb[i][:, kc, :],
                                                   start=(kc == 0), stop=(kc == 7)), ["wl", "gb%d" % i], ["ps2"])
            P.act(lambda e, i=i, pc=pc: e.activation(out=sq[i], in_=ps[:, pc, :], func=AF.Square), [pck], ["lsq%d" % i])
            P.pe(lambda e, i=i: e.matmul(ps[:, 3, :], self.ones[:, :], sq[i], start=True, stop=True),
                 ["lsq%d" % i, "ones"], ["ps3"])
            P.act(lambda e, i=i: e.activation(out=sr[i], in_=ps[:, 3, :], func=AF.Sqrt, bias=self.eps_t[:, 0:1],
                                              scale=1.0 / 128), ["ps3"], ["lsr%d" % i])
            P.dve(lambda e, i=i: e.reciprocal(out=rr[i], in_=sr[i]), ["lsr%d" % i], ["lrr%d" % i])
            P.dve(lambda e, i=i, pc=pc, sl=sl: e.scalar_tensor_tensor(
                out=ckvn[:, sl], in0=ps[:, pc, :], scalar=gkv[:, 0:1], in1=rr[i], op0=ALU.mult, op1=ALU.mult),
                [pck, "lrr%d" % i, "gains"], ["ckvn%d" % kb])
            self.rope_block(A, rb[i], ps[R, 2, :], "ps2", 512, rot[R, :], "rot", Ck[i][R, :], Sk[i][R, :],
                            ["rk%d" % i], KTm[R, sl], "KTpe%d" % kb, R, 4)
        P.barrier()
        A.release(m_lat)
        if self.stop == "lat":
            return
        Vh = [A.alloc([128, 64, 128], BF16) for _ in range(2)]
        P.pool(lambda e: e.memset(Vh[0][:, :, 64:128], 1.0), [], ["Vh0"])
        P.pool(lambda e: e.memset(Vh[1][:, :, 0:64], 1.0), [], ["Vh1"])
        QTh = [A.alloc([128, T], BF16) for _ in range(2)]
        P.pool(lambda e: e.memset(QTh[0][64:128, :], 0.0), [], ["QTh0"])
        P.pool(lambda e: e.memset(QTh[1][64:128, :], 0.0), [], ["QTh1"])
        PTm = [A.alloc([128, 512], BF16) for _ in range(3)]
        rzm = [A.alloc([128, 512], F32) for _ in range(2)]
        mixh = [A.alloc([128, T], BF16) for _ in range(2)]
        rb = self.rope_bufs(A, 2)
        scale = 1.0 / math.sqrt(96.0)
        pti = 0
        s_bank = 0
        rbi = 0
        for h in range(8):
            par = h % 2
            vsl = slice(0, 64) if par == 0 else slice(64, 128)
            zsl = slice(64, 128) if par == 0 else slice(0, 64)
            for kb in range(16):
                sl = slice(kb * 512, (kb + 1) * 512)
                pb = 5 + (kb % 2)
                pk = "ps%d" % pb
                P.pe(lambda e, h=h, sl=sl, pb=pb: e.matmul(ps[0:64, pb, :], wukv[:, h * 128:h * 128 + 64], ckvn[:, sl],
                                                          start=True, stop=True), ["wukv", "ckvn%d" % kb], [pk])
                self.evac(KTm[0:64, sl], ps[0:64, pb, :], [pk], ["KTm"], which="dve")
            if self.stop == "h0":
                return
            for k8 in range(8):
                pb = 5 + (k8 % 2)
                pk = "ps%d" % pb
                for j in range(8):
                    kt = k8 * 8 + j
                    P.pe(lambda e, h=h, kt=kt, j=j, pb=pb: e.matmul(
                        ps[:, pb, j * 64:(j + 1) * 64], ckvn[:, kt * 128:(kt + 1) * 128],
                        wukv[:, h * 128 + 64:h * 128 + 128], start=True, stop=True),
                        ["wukv", "ckvn%d" % (kt // 4)], [pk])
                P.dve(lambda e, k8=k8, pb=pb, par=par, vsl=vsl: e.tensor_copy(
                    out=Vh[par][:, k8 * 8:(k8 + 1) * 8, vsl],
                    in_=ps[:, pb, :].rearrange("p (j v) -> p j v", j=8)), [pk], ["Vh%d" % par])
            if self.stop == "h1":
                return
            qt = QTh[par]
            qk = "QTh%d" % par
            for b in range(4):
                sl = slice(b * 512, (b + 1) * 512)
                pb = 5 + (b % 2)
                pk = "ps%d" % pb
                for kc in range(2):
                    P.pe(lambda e, h=h, kc=kc, sl=sl, pb=pb: e.matmul(
                        ps[0:96, pb, :], wuq[:, kc, h * 96:h * 96 + 96], cqn[:, kc, sl], start=(kc == 0),
                        stop=(kc == 1)), ["wuq", "cqn%d" % b], [pk])
                self.evac(qt[0:64, sl], ps[0:64, pb, :], [pk], [qk], which="dve")
                self.rope_block(A, rb[rbi % 2], ps[R, pb, :], pk, 512, rot[R, :], "rot", Cq[R, sl], Sq[R, sl],
                                ["ropeq"], qt[R, sl], qk, R, 7, extra=[qk])
                rbi += 1
            if self.stop == "h2":
                return
            kpe_keys = ["KTpe%d" % kb for kb in range(16)]
            for qb in range(4):
                qsl = slice(qb * 512, (qb + 1) * 512)
                ob = 3 + (qb % 2)
                ok = "ps%d" % ob
                for kt in range(64):
                    sb = s_bank % 3
                    s_bank += 1
                    sk = "ps%d" % sb
                    P.pe(lambda e, kt=kt, qsl=qsl, sb=sb, qt=qt: e.matmul(
                        ps[:, sb, :], KTm[:, kt * 128:(kt + 1) * 128], qt[:, qsl], start=True, stop=True),
                        ["KTm", qk, kpe_keys[kt // 4]], [sk])
                    pt = PTm[pti % 3]
                    ptk = "PTm%d" % (pti % 3)
                    pti += 1
                    P.act(lambda e, pt=pt, sb=sb: e.activation(out=pt, in_=ps[:, sb, :], func=AF.Exp, scale=scale),
                          [sk], [ptk])
                    P.pe(lambda e, pt=pt, kt=kt, ob=ob, par=par: e.matmul(
                        ps[:, ob, :], Vh[par][:, kt, :], pt, start=(kt == 0), stop=(kt == 63)),
                        [ptk, "Vh%d" % par], [ok])
                if self.stop == "h3":
                    return
                rzz = rzm[qb % 2]
                rzk = "rzm%d" % (qb % 2)
                P.dve(lambda e, rzz=rzz, ob=ob, vsl=vsl, zsl=zsl: e.reciprocal(out=rzz[vsl, :], in_=ps[zsl, ob, :]),
                      [ok], [rzk])
                mh = mixh[(h // 2) % 2]
                mk = "mixh%d" % ((h // 2) % 2)
                P.dve(lambda e, rzz=rzz, ob=ob, vsl=vsl, qsl=qsl, mh=mh: e.tensor_tensor(
                    out=mh[vsl, qsl], in0=ps[vsl, ob, :], in1=rzz[vsl, :], op=ALU.mult), [ok, rzk], [mk])
            if par == 1:
                cch = 4 + h // 2
                self.mix_stores.append(P.dma("sp", lambda e, cch=cch, mh=mh: e.dma_start(
                    out=self.mixT[cch * 128:(cch + 1) * 128, :], in_=mh), [mk], ["mixT"]))


def na_offsets(qi):
    if qi == 0:
        return [-2, -1, 0, 1, 2, 3]
    if qi == 15:
        return [-3, -2, -1, 0, 1, 2]
    return [-2, -1, 0, 1, 2]


def na_slot(qi):
    return {0: 0, 1: 1, 14: 2, 15: 3}.get(qi, 4)


def vtile_col(g, ti):
    return (0, 17, 37)[g] + ti


def pc_layout(v, C):
    return np.ascontiguousarray(np.asarray(v, np.float32).reshape(C, 128).T)


def rope_table_np(npos_start, n, dim, rows):
    half = dim // 2
    inv = (1.0 / (10000.0 ** (np.arange(0, dim, 2, dtype=np.float32) / np.float32(dim)))).astype(np.float32)
    pos = np.arange(npos_start, npos_start + n, dtype=np.float32)
    pos = np.clip(pos, 0, S - 1)
    ang = (pos[None, :] * inv[:, None]).astype(np.float32)
    idx = np.arange(rows) % half
    return np.cos(ang)[idx].astype(np.float32), np.sin(ang)[idx].astype(np.float32)


def rot_matrix(nheads, dh):
    n = nheads * dh
    h = dh // 2
    M = np.zeros((n, n), np.float32)
    for b in range(nheads):
        for i in range(dh):
            if i < h:
                M[b * dh + i + h, b * dh + i] = -1.0
            else:
                M[b * dh + i - h, b * dh + i] = 1.0
    return M


def dil_masks():
    a = np.arange(128)[:, None]
    b = np.arange(128)[None, :]
    mA = (a >= b).astype(np.float32)
    mB = (a <= b).astype(np.float32)
    return np.ascontiguousarray(np.concatenate([mA, mA, mB, mB], axis=1))


def dil_vbias(j):
    vb = np.zeros((128, 72), np.float32)
    own0 = j * T
    for g, (w, d) in enumerate(DIL):
        L = T // d + 128
        ntile_ph = L // 128
        for p in range(d):
            for i in range(ntile_ph):
                m = i * 128 + np.arange(128)
                tok = own0 + (m - 64) * d + p
                col = vtile_col(g, p * ntile_ph + i)
                vb[:, col] = np.where((tok >= 0) & (tok < S), 0.0, NEG)
    return vb


def na_bias_tables(rpb, j):
    out = np.full((5, 4, 128, 6, 256), NEG, np.float32)
    rpb = np.asarray(rpb, np.float32)
    reps = {0: 0, 1: 1, 2: 14, 3: 15, 4: 7}
    a = np.arange(128)
    for slot, qi in reps.items():
        gq = j * 16 + qi
        qtok = gq * 128 + np.arange(128)
        r = qtok // 64
        cc = qtok % 64
        rs = np.clip(r - 4, 0, 128 - 8)
        cs = np.clip(cc - 8, 0, 64 - 16)
        offs = na_offsets(qi)
        for jj, off in enumerate(offs):
            gk = gq + off
            if gk < 0 or gk >= 64:
                continue
            ktok = gk * 128 + a
            kr = ktok // 64
            kc = ktok % 64
            valid = ((kr[:, None] >= rs[None, :]) & (kr[:, None] < rs[None, :] + 8) &
                     (kc[:, None] >= cs[None, :]) & (kc[:, None] < cs[None, :] + 16))
            rr = np.clip(kr[:, None] - r[None, :] + 7, 0, 14)
            cr = np.clip(kc[:, None] - cc[None, :] + 15, 0, 30)
            for h in range(8):
                vals = rpb[h][rr, cr]
                tile = np.where(valid, vals, np.float32(NEG))
                out[slot, h // 2, :, jj, (h % 2) * 128:(h % 2 + 1) * 128] = tile
    return out


_PROGS = {}


def get_prog(kind):
    if kind not in _PROGS:
        b = Builder(kind)
        _PROGS[kind] = b.build()
    return _PROGS[kind]


def run(kind, in_maps):
    nc = get_prog(kind)
    res = run_bass_kernel_spmd(nc, in_maps, core_ids=list(range(NCORES)))
    return res.results


def make_exchange(xn_list):
    galls = []
    for b in range(2):
        galls.append(np.ascontiguousarray(np.concatenate([xn_list[b * 4 + j] for j in range(4)], axis=1)))
    gexts = []
    for c in range(NCORES):
        b, j = divmod(c, 4)
        ge = np.zeros((D, EXT), dtype=galls[b].dtype)
        lo = j * T - HALO
        hi = (j + 1) * T + HALO
        slo, shi = max(lo, 0), min(hi, S)
        ge[:, slo - lo:shi - lo] = galls[b][:, slo:shi]
        gexts.append(ge)
    return gexts, galls


def kernel(x, norm_mix, norm_mlp, norm_final, ev_w_in, ev_rpb, ev_q_norm, ev_w_uq, ev_kv_norm, ev_w_ukv,
           ev_w_o, od_w_in, od_w_o, mlp_w1, mlp_w2):
    x = np.asarray(x, np.float32)
    f = lambda a: np.ascontiguousarray(np.asarray(a, np.float32))
    xT = []
    for c in range(NCORES):
        b, j = divmod(c, 4)
        xT.append(np.ascontiguousarray(x[b, j * T:(j + 1) * T, :].T))
    ident = np.eye(128, dtype=np.float32)
    res = run("norm0", [{"xT": xT[c], "gnext": pc_layout(norm_mix[0], 8)} for c in range(NCORES)])
    xn = [res[c]["xn_bf"] for c in range(NCORES)]
    final = None
    for layer in range(4):
        gexts, galls = make_exchange(xn)
        gnext = norm_mix[layer + 1] if layer < 3 else norm_final
        mix_maps = []
        tail_maps = []
        for c in range(NCORES):
            b, j = divmod(c, 4)
            tm = {"xT": xT[c], "w1": f(mlp_w1[layer]), "w2": f(mlp_w2[layer]),
                  "g2": pc_layout(norm_mlp[layer], 8), "gnext": pc_layout(gnext, 8)}
            m = {"gext": gexts[c], "ident": ident}
            if layer % 2 == 0:
                e = layer // 2
                rkC, rkS = rope_table_np(0, S, 32, 32)
                rqC, rqS = rope_table_np(j * T, T, 32, 32)
                m.update({"gall": galls[b], "w_in": f(ev_w_in[e]), "w_uq": f(ev_w_uq[e]), "w_ukv": f(ev_w_ukv[e]),
                          "qn": pc_layout(ev_q_norm[e], 2), "kvn": pc_layout(ev_kv_norm[e], 1),
                          "nab": na_bias_tables(ev_rpb[e], j), "rkC": rkC, "rkS": rkS, "rqC": rqC, "rqS": rqS,
                          "rot32": rot_matrix(1, 32)})
                tm["w_o"] = f(ev_w_o[e])
            else:
                o = layer // 2
                rC, rS = rope_table_np(j * T - HALO, EXT, 64, 128)
                m.update({"w_in": f(od_w_in[o]), "ropeC": rC, "ropeS": rS,
                          "rot64": rot_matrix(2, 64), "vbias": dil_vbias(j), "masks": dil_masks()})
                tm["w_o"] = f(od_w_o[o])
            mix_maps.append(m)
            tail_maps.append(tm)
        for c in range(NCORES):
            mix_maps[c].update(tail_maps[c])
        res = run("even" if layer % 2 == 0 else "odd", mix_maps)
        xT = [res[c]["xoT"] for c in range(NCORES)]
        xn = [res[c]["xn_bf"] for c in range(NCORES)]
        final = [res[c]["xn_f32"] for c in range(NCORES)]
    out = np.empty((2, S, D), np.float32)
    for c in range(NCORES):
        b, j = divmod(c, 4)
        out[b, j * T:(j + 1) * T, :] = final[c].T
    return out
```
